# Optimizing a Trainium2 kernel written in Bass

```python
import math
import jax, jax.numpy as jnp
from jax import lax
import numpy as np

D_MODEL = 4096
BATCH = 4
SEQ = 4096
DEPTH = 4

GRID_W = 64
CTX_LEN = 256
N_MIXERS = 3
HEAD_DIM = 128
A_HEADS = D_MODEL // HEAD_DIM
A_KV_HEADS = A_HEADS // 4
B_HEADS = D_MODEL // HEAD_DIM
B_WIN_R = 8
B_WIN_C = 16
B_COL_BLOCK = 16
B_KEY_COLS = 32
C_HEADS = D_MODEL // (2 * HEAD_DIM)
MLP_HIDDEN = 4 * D_MODEL
MOD_RANK = 1024
N_MOD = 6
Q_BLOCK = 128
ROPE_THETA = 10000.0
EPS = 1e-6
N_A_LAYERS = (DEPTH + N_MIXERS - 1) // N_MIXERS
N_B_LAYERS = (DEPTH + N_MIXERS - 2) // N_MIXERS
N_C_LAYERS = DEPTH // N_MIXERS

kernel_name = "hybrid_dit_gqa_natten_diffattn"


def rms_norm(x, gain):
    x32 = x.astype(jnp.float32)
    y = x32 * lax.rsqrt(jnp.mean(x32 * x32, axis=-1, keepdims=True) + EPS)
    return y.astype(x.dtype) * gain


def modulate(u, shift, scale):
    return u * (1 + scale) + shift


def axial_rope_tables(n_tokens, dim):
    t = jnp.arange(n_tokens, dtype=jnp.int32)
    row = (t // GRID_W).astype(jnp.float32)
    col = (t % GRID_W).astype(jnp.float32)
    n_freq = dim // 4
    inv_freq = ROPE_THETA ** (-jnp.arange(n_freq, dtype=jnp.float32) / n_freq)
    ang = jnp.concatenate([row[:, None] * inv_freq, col[:, None] * inv_freq], axis=-1)
    return jnp.cos(ang), jnp.sin(ang)


def apply_rope(x, cos, sin):
    half = x.shape[-1] // 2
    bshape = (1, x.shape[1]) + (1,) * (x.ndim - 3) + (half,)
    c = cos.reshape(bshape).astype(x.dtype)
    s = sin.reshape(bshape).astype(x.dtype)
    xp = x.reshape(*x.shape[:-1], half, 2)
    x0, x1 = xp[..., 0], xp[..., 1]
    return jnp.stack([x0 * c - x1 * s, x0 * s + x1 * c], axis=-1).reshape(x.shape)


def sweep_query_blocks(block_fn, *qs):
    b, s = qs[0].shape[:2]
    nb = s // Q_BLOCK
    blocks = tuple(jnp.moveaxis(q.reshape(b, nb, Q_BLOCK, *q.shape[2:]), 1, 0) for q in qs)
    out = lax.map(lambda qb: block_fn(*qb), blocks)
    return jnp.moveaxis(out, 0, 1).reshape(b, s, *out.shape[3:])


def gqa_attend(q, k, v):
    s = jnp.einsum('bqhgd,bkhd->bhgqk', q, k, preferred_element_type=jnp.float32) * (q.shape[-1] ** -0.5)
    p = jax.nn.softmax(s, axis=-1).astype(v.dtype)
    return jnp.einsum('bhgqk,bkhd->bqhgd', p, v)


def channel_mixer(u, w1, w2):
    return jnp.square(jax.nn.relu(u @ w1)) @ w2


def mixer_gqa(h_lat, h_ctx, w_qkv, w_o, q_gain, k_gain, cos, sin, with_ctx_out):
    n_q, n_kv = A_HEADS * HEAD_DIM, A_KV_HEADS * HEAD_DIM
    group = A_HEADS // A_KV_HEADS

    def project(h):
        y = h @ w_qkv
        bb, n = h.shape[:2]
        q = rms_norm(y[..., :n_q].reshape(bb, n, A_KV_HEADS, group, HEAD_DIM), q_gain)
        k = rms_norm(y[..., n_q:n_q + n_kv].reshape(bb, n, A_KV_HEADS, HEAD_DIM), k_gain)
        v = y[..., n_q + n_kv:].reshape(bb, n, A_KV_HEADS, HEAD_DIM)
        return q, k, v

    q_l, k_l, v_l = project(h_lat)
    q_c, k_c, v_c = project(h_ctx)
    q_l = apply_rope(q_l, cos, sin)
    k_l = apply_rope(k_l, cos, sin)
    keys = jnp.concatenate([k_c, k_l], axis=1)
    vals = jnp.concatenate([v_c, v_l], axis=1)
    o_l = sweep_query_blocks(lambda qb: gqa_attend(qb, keys, vals), q_l)
    out_l = o_l.reshape(h_lat.shape[0], h_lat.shape[1], n_q) @ w_o
    out_c = None
    if with_ctx_out:
        out_c = gqa_attend(q_c, k_c, v_c).reshape(h_ctx.shape[0], h_ctx.shape[1], n_q) @ w_o
    return out_l, out_c


def mixer_neighbourhood(h_lat, h_ctx, w_qkv, w_o, rpb, with_ctx_out):
    b, s, _ = h_lat.shape
    rows = s // GRID_W
    wr = min(B_WIN_R, rows)
    n_h = B_HEADS * HEAD_DIM
    scale = HEAD_DIM ** -0.5

    def project(h):
        y = h @ w_qkv
        bb, n = h.shape[:2]
        return tuple(y[..., i * n_h:(i + 1) * n_h].reshape(bb, n, B_HEADS, HEAD_DIM) for i in range(3))

    q_l, k_l, v_l = project(h_lat)
    q_c, k_c, v_c = project(h_ctx)
    grid = (b, rows, GRID_W, B_HEADS, HEAD_DIM)
    qg, kg, vg = q_l.reshape(grid), k_l.reshape(grid), v_l.reshape(grid)
    n_cb = GRID_W // B_COL_BLOCK
    n_win = wr * B_KEY_COLS
    q_col_off = jnp.arange(B_COL_BLOCK, dtype=jnp.int32)
    key_col_off = jnp.arange(B_KEY_COLS, dtype=jnp.int32)
    key_row_off = jnp.arange(wr, dtype=jnp.int32)

    def block(idx):
        r = idx // n_cb
        j = idx % n_cb
        r0 = jnp.clip(r - B_WIN_R // 2, 0, rows - wr)
        qc = j * B_COL_BLOCK + q_col_off
        c0 = jnp.clip(qc - B_WIN_C // 2, 0, GRID_W - B_WIN_C)
        cs = jnp.clip(j * B_COL_BLOCK - (B_KEY_COLS - B_COL_BLOCK) // 2, 0, GRID_W - B_KEY_COLS)
        qb = lax.dynamic_slice(qg, (0, r, j * B_COL_BLOCK, 0, 0), (b, 1, B_COL_BLOCK, B_HEADS, HEAD_DIM))[:, 0]
        kb = lax.dynamic_slice(kg, (0, r0, cs, 0, 0), (b, wr, B_KEY_COLS, B_HEADS, HEAD_DIM))
        vb = lax.dynamic_slice(vg, (0, r0, cs, 0, 0), (b, wr, B_KEY_COLS, B_HEADS, HEAD_DIM))
        kc = cs + key_col_off
        kr = r0 + key_row_off
        in_win = (kc[None, :] >= c0[:, None]) & (kc[None, :] < c0[:, None] + B_WIN_C)
        dr = kr - r + (B_WIN_R - 1)
        dc = jnp.clip(kc[None, :] - qc[:, None] + (B_WIN_C - 1), 0, 2 * B_WIN_C - 2)
        bias = rpb[:, dr[None, :, None], dc[:, None, :]]
        s_win = jnp.einsum('bqhd,brkhd->bhqrk', qb, kb, preferred_element_type=jnp.float32) * scale + bias
        s_win = jnp.where(in_win[:, None, :], s_win, -jnp.inf).reshape(b, B_HEADS, B_COL_BLOCK, n_win)
        s_ctx = jnp.einsum('bqhd,bkhd->bhqk', qb, k_c, preferred_element_type=jnp.float32) * scale
        p = jax.nn.softmax(jnp.concatenate([s_win, s_ctx], axis=-1), axis=-1).astype(v_l.dtype)
        o = jnp.einsum('bhqk,bkhd->bqhd', p[..., :n_win], vb.reshape(b, n_win, B_HEADS, HEAD_DIM))
        return o + jnp.einsum('bhqk,bkhd->bqhd', p[..., n_win:], v_c)

    out = lax.map(block, jnp.arange(rows * n_cb, dtype=jnp.int32))
    out = out.reshape(rows, n_cb, b, B_COL_BLOCK, n_h).transpose(2, 0, 1, 3, 4).reshape(b, s, n_h)
    out_l = out @ w_o
    out_c = None
    if with_ctx_out:
        o_c = gqa_attend(q_c[:, :, :, None], k_c, v_c)[:, :, :, 0]
        out_c = o_c.reshape(h_ctx.shape[0], h_ctx.shape[1], n_h) @ w_o
    return out_l, out_c


def mixer_diff(h_lat, h_ctx, w_qkv, w_o, lq1, lk1, lq2, lk2, subln_g, cos, sin, lambda_init, with_ctx_out):
    n_qk = C_HEADS * 2 * HEAD_DIM
    scale = HEAD_DIM ** -0.5

    def project(h):
        y = h @ w_qkv
        bb, n = h.shape[:2]
        q = y[..., :n_qk].reshape(bb, n, C_HEADS, 2, HEAD_DIM)
        k = y[..., n_qk:2 * n_qk].reshape(bb, n, C_HEADS, 2, HEAD_DIM)
        v = y[..., 2 * n_qk:].reshape(bb, n, C_HEADS, 2 * HEAD_DIM)
        return q, k, v

    lam = (jnp.exp(jnp.sum(lq1.astype(jnp.float32) * lk1.astype(jnp.float32)))
           - jnp.exp(jnp.sum(lq2.astype(jnp.float32) * lk2.astype(jnp.float32))) + lambda_init)

    def attend(q, k, v):
        s = jnp.einsum('bqhcd,bkhcd->bchqk', q, k, preferred_element_type=jnp.float32) * scale
        p = jax.nn.softmax(s, axis=-1)
        a = (p[:, 0] - lam * p[:, 1]).astype(v.dtype)
        o = jnp.einsum('bhqk,bkhd->bqhd', a, v)
        return rms_norm(o, subln_g) * (1 - lambda_init)

    q_l, k_l, v_l = project(h_lat)
    q_c, k_c, v_c = project(h_ctx)
    q_l = apply_rope(q_l, cos, sin)
    k_l = apply_rope(k_l, cos, sin)
    keys = jnp.concatenate([k_c, k_l], axis=1)
    vals = jnp.concatenate([v_c, v_l], axis=1)
    o_l = sweep_query_blocks(lambda qb: attend(qb, keys, vals), q_l)
    out_l = o_l.reshape(h_lat.shape[0], h_lat.shape[1], D_MODEL) @ w_o
    out_c = None
    if with_ctx_out:
        out_c = attend(q_c, k_c, v_c).reshape(h_ctx.shape[0], h_ctx.shape[1], D_MODEL) @ w_o
    return out_l, out_c


def setup_inputs(seed: int = 0) -> dict:
    key = jax.random.key(seed)
    ks = jax.random.split(key, 32)
    counter = [0]

    def nrm(shape, scale):
        k = ks[counter[0]]
        counter[0] += 1
        return jax.random.normal(k, shape, jnp.float32) * scale

    d = D_MODEL
    qkv_a = (A_HEADS + 2 * A_KV_HEADS) * HEAD_DIM
    return {
        "x": nrm((BATCH, SEQ, d), 1.0),
        "c": nrm((BATCH, d), 1.0),
        "ctx": nrm((BATCH, CTX_LEN, d), 1.0),
        "c_ctx": nrm((d,), 1.0),
        "norm1_g": 1.0 + nrm((DEPTH, d), 0.02),
        "norm2_g": 1.0 + nrm((DEPTH, d), 0.02),
        "mod_down": nrm((DEPTH, d, MOD_RANK), d ** -0.5),
        "mod_up": nrm((DEPTH, MOD_RANK, N_MOD * d), 0.3 * MOD_RANK ** -0.5),
        "mod_b": nrm((DEPTH, N_MOD * d), 0.02),
        "mlp_w1": nrm((DEPTH, d, MLP_HIDDEN), d ** -0.5),
        "mlp_w2": nrm((DEPTH, MLP_HIDDEN, d), MLP_HIDDEN ** -0.5),
        "a_w_qkv": nrm((N_A_LAYERS, d, qkv_a), d ** -0.5),
        "a_w_o": nrm((N_A_LAYERS, A_HEADS * HEAD_DIM, d), (A_HEADS * HEAD_DIM) ** -0.5),
        "a_q_g": 1.0 + nrm((N_A_LAYERS, HEAD_DIM), 0.02),
        "a_k_g": 1.0 + nrm((N_A_LAYERS, HEAD_DIM), 0.02),
        "b_w_qkv": nrm((N_B_LAYERS, d, 3 * B_HEADS * HEAD_DIM), d ** -0.5),
        "b_w_o": nrm((N_B_LAYERS, B_HEADS * HEAD_DIM, d), (B_HEADS * HEAD_DIM) ** -0.5),
        "b_rpb": nrm((N_B_LAYERS, B_HEADS, 2 * B_WIN_R - 1, 2 * B_WIN_C - 1), 0.1),
        "c_w_qkv": nrm((N_C_LAYERS, d, 3 * d), d ** -0.5),
        "c_w_o": nrm((N_C_LAYERS, d, d), d ** -0.5),
        "c_lam_q1": nrm((N_C_LAYERS, HEAD_DIM), 0.1),
        "c_lam_k1": nrm((N_C_LAYERS, HEAD_DIM), 0.1),
        "c_lam_q2": nrm((N_C_LAYERS, HEAD_DIM), 0.1),
        "c_lam_k2": nrm((N_C_LAYERS, HEAD_DIM), 0.1),
        "c_subln_g": 1.0 + nrm((N_C_LAYERS, 2 * HEAD_DIM), 0.02),
        "final_g": 1.0 + nrm((d,), 0.02),
    }


def reference(x, c, ctx, c_ctx, norm1_g, norm2_g, mod_down, mod_up, mod_b, mlp_w1, mlp_w2,
              a_w_qkv, a_w_o, a_q_g, a_k_g, b_w_qkv, b_w_o, b_rpb,
              c_w_qkv, c_w_o, c_lam_q1, c_lam_k1, c_lam_q2, c_lam_k2, c_subln_g, final_g):
    cos, sin = axial_rope_tables(x.shape[1], HEAD_DIM)
    silu_c = jax.nn.silu(c)
    silu_cc = jax.nn.silu(c_ctx)
    h_lat, h_ctx = x, ctx
    for i in range(DEPTH):
        with_ctx_out = i < DEPTH - 1
        mod_l = (silu_c @ mod_down[i]) @ mod_up[i] + mod_b[i]
        mod_c = (silu_cc @ mod_down[i]) @ mod_up[i] + mod_b[i]
        sh1, sc1, g1, sh2, sc2, g2 = jnp.split(mod_l, N_MOD, axis=-1)
        cmod = jnp.split(mod_c, N_MOD, axis=-1)
        u_lat = modulate(rms_norm(h_lat, norm1_g[i]), sh1[:, None], sc1[:, None])
        u_ctx = modulate(rms_norm(h_ctx, norm1_g[i]), cmod[0], cmod[1])
        kind, slot = i % N_MIXERS, i // N_MIXERS
        if kind == 0:
            out_l, out_c = mixer_gqa(u_lat, u_ctx, a_w_qkv[slot], a_w_o[slot], a_q_g[slot], a_k_g[slot],
                                     cos, sin, with_ctx_out)
        elif kind == 1:
            out_l, out_c = mixer_neighbourhood(u_lat, u_ctx, b_w_qkv[slot], b_w_o[slot], b_rpb[slot], with_ctx_out)
        else:
            lambda_init = 0.8 - 0.6 * math.exp(-0.3 * i)
            out_l, out_c = mixer_diff(u_lat, u_ctx, c_w_qkv[slot], c_w_o[slot], c_lam_q1[slot], c_lam_k1[slot],
                                      c_lam_q2[slot], c_lam_k2[slot], c_subln_g[slot], cos, sin,
                                      lambda_init, with_ctx_out)
        h_lat = h_lat + g1[:, None] * out_l
        v_lat = modulate(rms_norm(h_lat, norm2_g[i]), sh2[:, None], sc2[:, None])
        h_lat = h_lat + g2[:, None] * channel_mixer(v_lat, mlp_w1[i], mlp_w2[i])
        if with_ctx_out:
            h_ctx = h_ctx + cmod[2] * out_c
            v_ctx = modulate(rms_norm(h_ctx, norm2_g[i]), cmod[3], cmod[4])
            h_ctx = h_ctx + cmod[5] * channel_mixer(v_ctx, mlp_w1[i], mlp_w2[i])
    return rms_norm(h_lat, final_g)
```

```python
import math
from contextlib import ExitStack
import numpy as np
import concourse.bass as bass
import concourse.mybir as mybir
from concourse.bass_utils import run_bass_kernel_spmd

F32 = mybir.dt.float32
BF16 = mybir.dt.bfloat16
ALU = mybir.AluOpType
AF = mybir.ActivationFunctionType
NEG = -30000.0
EPS = 1e-6


class Cfg:
    def __init__(self, D=4096, B=4, S=4096, CTX=256, DEPTH=4, HID=None, RANK=1024, GW=64):
        self.D, self.B, self.S, self.CTX, self.DEPTH, self.RANK, self.GW = D, B, S, CTX, DEPTH, RANK, GW
        self.HID = HID or 4 * D
        self.HD = 128
        self.KC = D // 128
        self.AH = D // 128
        self.AKV = self.AH // 4
        self.BH = D // 128
        self.CH = D // 256
        self.NB = CTX + S
        self.NTOK = B * self.NB
        self.ROWS = S // GW
        self.NQKV = [(self.AH + 2 * self.AKV) * 128, 3 * D, 3 * D]
        self.NQKVMAX = 3 * D
        self.NG = B + 1
        assert self.NTOK % 512 == 0 and S % 512 == 0 and CTX == 256 and GW == 64


class Res:
    __slots__ = ("name", "w", "r")

    def __init__(self, name=""):
        self.name = name
        self.w = None
        self.r = {}


class Slot(Res):
    __slots__ = ("tr", "sems", "cnts")

    def __init__(self, tr, name=""):
        super().__init__(name)
        self.tr = tr
        self.sems = {}
        self.cnts = {}

    def sem_for(self, kind):
        if kind not in self.sems:
            self.sems[kind] = self.tr.new_sem("d%s_%s" % (kind, self.name))
            self.cnts[kind] = 0
        return self.sems[kind]


class Eng:
    def __init__(self, tr, name, h, is_pe=False):
        self.name, self.h, self.is_pe = name, h, is_pe
        self.sem = tr.new_sem("e_" + name)
        self.cnt = 0
        self.known = {}

    def wait(self, ev):
        if ev is None:
            return
        sem, c = ev
        if self.is_pe and sem is self.sem:
            return
        k = id(sem)
        if self.known.get(k, 0) >= c:
            return
        self.h.wait_ge(sem, c)
        self.known[k] = c


class Tracker:
    def __init__(self, nc):
        self.nc = nc
        self.sems = []
        self.pe = Eng(self, "pe", nc.tensor, True)
        self.act = Eng(self, "act", nc.scalar)
        self.dve = Eng(self, "dve", nc.vector)
        self.pool = Eng(self, "pool", nc.gpsimd)
        self.sp = Eng(self, "sp", nc.sync)
        self.engs = [self.pe, self.act, self.dve, self.pool, self.sp]
        self.cur = {}
        self.free_slots = []

    def new_sem(self, name):
        s = self.nc.alloc_semaphore(name=name + "_%d" % len(self.sems))
        self.sems.append(s)
        return s

    def _pre(self, eng, reads, writes):
        for r in reads:
            eng.wait(r.w)
        for w in writes:
            eng.wait(w.w)
            for ev in w.r.values():
                eng.wait(ev)

    def _post(self, ev, reads, writes):
        self.cur[id(ev[0])] = ev
        for r in reads:
            r.r[id(ev[0])] = ev
        for w in writes:
            w.w = ev
            w.r = {}

    def op(self, eng, emit, reads=(), writes=()):
        self._pre(eng, reads, writes)
        inst = emit()
        if isinstance(inst, (list, tuple)):
            inst = inst[-1]
        eng.cnt += 1
        inst.then_inc(eng.sem, 1)
        self._post((eng.sem, eng.cnt), reads, writes)

    def dma(self, q, slot, pairs, reads=(), writes=(), **kw):
        if any(o.dtype != i.dtype for (o, i) in pairs):
            q = self.pool
        self._pre(q, reads, writes)
        kind = "sw" if q is self.pool else "hw"
        sem = slot.sem_for(kind)
        self.ndma = getattr(self, "ndma", {})
        self.ndma[q.name] = self.ndma.get(q.name, 0) + len(pairs)
        for (o, i) in pairs:
            q.h.dma_start(out=o, in_=i, **kw).then_inc(sem, 16)
            slot.cnts[kind] += 16
        self._post((sem, slot.cnts[kind]), reads, writes)

    def barrier(self):
        for e in self.engs:
            for ev in list(self.cur.values()):
                e.wait(ev)


class Ring:
    def __init__(self, items):
        self.items = items
        self.i = 0

    def next(self):
        it = self.items[self.i % len(self.items)]
        self.i += 1
        return it


class RowSplit:
    def __init__(self, prog, name, rows, cols, dt, npieces):
        self.rows, self.pr = rows, rows // npieces
        assert self.pr * npieces == rows and self.pr % 128 == 0
        self.aps = [prog.dram("%s_%d" % (name, i), [self.pr, cols], dt) for i in range(npieces)]

    def __getitem__(self, key):
        rs, cs = key
        r0 = rs.start or 0
        r1 = self.rows if rs.stop is None else rs.stop
        i = r0 // self.pr
        assert i == (r1 - 1) // self.pr, (r0, r1, self.pr)
        return self.aps[i][r0 - i * self.pr:r1 - i * self.pr, cs]

    def all_rows(self, cs):
        return [(i * self.pr, (i + 1) * self.pr, ap[:, cs]) for i, ap in enumerate(self.aps)]


class Buf:
    def __init__(self, t, res):
        self.t = t
        self.res = res


class Prog:
    def __init__(self, cfg):
        self.cfg = cfg
        self.nc = bass.Bass("TRN2", target_bir_lowering=False)
        self.tr = Tracker(self.nc)
        self.slot_pool = []
        self.uid = 0

    def name(self, s):
        self.uid += 1
        return "%s_%d" % (s, self.uid)

    def dram(self, name, shape, dt, kind="Internal"):
        if kind == "Internal" and getattr(self.cfg, "debug", False) and name.split("_")[0] in ("hT", "uT", "qkvT", "attnT", "HT"):
            kind = "ExternalOutput"
        return self.nc.dram_tensor(name, list(shape), dt, kind=kind).ap()

    def get_slot(self, name):
        if self.slot_pool:
            s = self.slot_pool.pop()
            s.name = name
            return s
        return Slot(self.tr, name)

    def sb(self, es, shape, dt, name, slot=False, track=None):
        t = es.enter_context(self.nc.sbuf_tensor(self.name(name), list(shape), dt))
        if slot:
            r = self.get_slot(name)
            es.callback(self._release_slot, r)
        else:
            r = Res(name)
        return Buf(t, r)

    def _release_slot(self, s):
        s.w = None
        s.r = {}
        self.slot_pool.append(s)

    def chunk_pairs(self, sb_t, dr, t0, tw, load, step=8):
        pairs = []
        step = min(step, dr.pr // 128)
        for c0 in range(0, dr.rows // 128, step):
            d = dr[c0 * 128:(c0 + step) * 128, t0:t0 + tw].rearrange("(c p) t -> p c t", p=128)
            sbv = sb_t[:, c0:c0 + step, :]
            pairs.append((sbv, d) if load else (d, sbv))
        return pairs

    def col_pairs(self, dst_ap, src_1d, ncols, step=8):
        return [(dst_ap[:, c0:min(c0 + step, ncols)],
                 src_1d[c0 * 128:min(c0 + step, ncols) * 128].rearrange("(c p) -> p c", p=128))
                for c0 in range(0, ncols, step)]

    def ps(self, es, name, dt=F32, cols=512):
        t = es.enter_context(self.nc.psum_tensor(self.name(name), [128, cols], dt))
        return Buf(t, Res(name))

    def build(self):
        cfg, nc, tr = self.cfg, self.nc, self.tr
        D, B, S, CTX, KC, NTOK, NB, NG = cfg.D, cfg.B, cfg.S, cfg.CTX, cfg.KC, cfg.NTOK, cfg.NB, cfg.NG
        HID, RANK, DEPTH = cfg.HID, cfg.RANK, cfg.DEPTH
        RC = RANK // 128
        nA = (DEPTH + 2) // 3
        nBl = (DEPTH + 1) // 3
        nC = DEPTH // 3
        I = {}

        def ext(name, shape, dt=F32):
            I[name] = self.dram(name, shape, dt, kind="ExternalInput")
            return I[name]

        ext("x", [B * S, D]); ext("c", [B, D]); ext("ctx", [B * CTX, D]); ext("c_ctx", [1, D])
        ext("norm1_g", [DEPTH, D]); ext("norm2_g", [DEPTH, D])
        ext("mod_down", [DEPTH * D, RANK]); ext("mod_up", [DEPTH * RANK, 6 * D]); ext("mod_b", [DEPTH, 6 * D])
        ext("mlp_w1", [DEPTH * D, HID]); ext("mlp_w2", [DEPTH * HID, D])
        ext("a_w_qkv", [nA * D, cfg.NQKV[0]]); ext("a_w_o", [nA * D, D])
        ext("a_q_g", [nA, 128]); ext("a_k_g", [nA, 128])
        ext("b_w_qkv", [max(nBl, 1) * D, 3 * D]); ext("b_w_o", [max(nBl, 1) * D, D])
        ext("rpbx", [max(nBl, 1) * cfg.BH * 18, 64 * 64])
        ext("c_w_qkv", [max(nC, 1) * D, 3 * D]); ext("c_w_o", [max(nC, 1) * D, D])
        ext("c_lam", [max(nC, 1) * 4, 128]); ext("c_subln_g", [max(nC, 1), 256])
        ext("final_g", [1, D])
        ext("k_ident", [128, 128]); ext("k_ones", [128, 128]); ext("k_swap", [128, 128])
        ext("k_cos", [128, NTOK]); ext("k_sin", [128, NTOK])
        ext("k_cm", [128, 64]); ext("k_rm", [128, 4])
        out = self.dram("out", [B * S, D], F32, kind="ExternalOutput")

        big = D * NTOK * 4 > 200 * 2 ** 20
        hT = RowSplit(self, "hT", D, NTOK, F32, 2 if big else 1)
        uT = RowSplit(self, "uT", D, NTOK, BF16, 1)
        qkvT = RowSplit(self, "qkvT", cfg.NQKVMAX, NTOK, BF16, 3 if big else 1)
        attnT = RowSplit(self, "attnT", D, NTOK, BF16, 1)
        HT = RowSplit(self, "HT", HID, NTOK, BF16, HID // D)
        self.I, self.hT, self.uT, self.qkvT, self.attnT, self.HT, self.out = I, hT, uT, qkvT, attnT, HT, out

        def wt(name, K, N):
            return self.dram(name, [N // 256, 128, (K // 128) * 256], BF16)

        W = []
        for i in range(DEPTH):
            kind, slot = i % 3, i // 3
            d = {}
            d["qkv"] = wt("wqkv%d" % i, D, cfg.NQKV[kind])
            d["o"] = wt("wo%d" % i, D, D)
            d["w1"] = wt("w1_%d" % i, D, HID)
            d["w2"] = [wt("w2_%d_%d" % (i, j), D, D) for j in range(HID // D)]
            d["md"] = wt("md%d" % i, D, RANK)
            d["mu"] = wt("mu%d" % i, RANK, 6 * D)
            W.append(d)
        self.W = W

        with ExitStack() as ges:
            self.ident = self.sb(ges, [128, 128], F32, "ident", slot=True)
            self.identb = self.sb(ges, [128, 128], BF16, "identb", slot=True)
            self.onesb = self.sb(ges, [128, 128], BF16, "onesb", slot=True)
            self.swapb = self.sb(ges, [128, 128], BF16, "swapb", slot=True)
            self.sT = self.sb(ges, [128, KC, NG], BF16, "sT")
            self.modT = self.sb(ges, [128, 6 * KC, NG], F32, "modT")
            self.gs1 = self.sb(ges, [128, KC, NG], F32, "gs1")
            self.gs2 = self.sb(ges, [128, KC, NG], F32, "gs2")
            self.n1g = self.sb(ges, [128, KC], F32, "n1g", slot=True)
            self.n2g = self.sb(ges, [128, KC], F32, "n2g", slot=True)
            self.fgT = self.sb(ges, [128, KC], F32, "fgT", slot=True)
            self.mbT = self.sb(ges, [128, 6 * KC], F32, "mbT", slot=True)
            self.epsT = self.sb(ges, [128, 1], F32, "epsT")
            self.dslot = self.get_slot("d2d")

            sp = tr.sp
            tr.dma(sp, self.ident.res, [(self.ident.t[:], I["k_ident"][:, :])], writes=[self.ident.res])
            tr.dma(sp, self.identb.res, [(self.identb.t[:], I["k_ident"][:, :])], writes=[self.identb.res])
            tr.dma(sp, self.onesb.res, [(self.onesb.t[:], I["k_ones"][:, :])], writes=[self.onesb.res])
            tr.dma(sp, self.swapb.res, [(self.swapb.t[:], I["k_swap"][:, :])], writes=[self.swapb.res])
            tr.dma(sp, self.fgT.res, self.col_pairs(self.fgT.t, I["final_g"][0, :], KC),
                   writes=[self.fgT.res], allow_slow_non_contiguous=True)
            tr.op(tr.dve, lambda: nc.vector.memset(self.epsT.t[:], EPS), writes=[self.epsT.res])

            import os
            stop = int(os.environ.get("KSTOP", "100000"))
            steps = [self.phase_cast_weights, self.phase_silu, self.phase_in_transpose, tr.barrier]
            for i in range(DEPTH):
                steps += self.layer_steps(i)
            steps += [self.phase_final]
            for k, f in enumerate(steps):
                if k >= stop:
                    break
                f()
            tr.barrier()
        return nc

    def phase_cast_weights(self):
        cfg, tr, I = self.cfg, self.tr, self.I
        D, HID, RANK, DEPTH = cfg.D, cfg.HID, cfg.RANK, cfg.DEPTH
        qs = [tr.sp, tr.pool]
        self._ci = 0

        def cast(dst, src, K, N):
            kc = K // 128
            step = max(1, min(kc, 8))
            for g in range(N // 256):
                for c0 in range(0, kc, step):
                    o = dst[g, :, c0 * 256:(c0 + step) * 256].rearrange("p (c n) -> p c n", n=256)
                    i = src[c0 * 128:(c0 + step) * 128, g * 256:(g + 1) * 256].rearrange("(c p) n -> p c n", p=128)
                    q = qs[self._ci % 2]
                    self._ci += 1
                    tr.dma(q, self.dslot, [(o, i)])

        for i in range(DEPTH):
            kind, slot = i % 3, i // 3
            wq = [I["a_w_qkv"], I["b_w_qkv"], I["c_w_qkv"]][kind]
            wo = [I["a_w_o"], I["b_w_o"], I["c_w_o"]][kind]
            cast(self.W[i]["qkv"], wq[slot * D:(slot + 1) * D, :], D, cfg.NQKV[kind])
            cast(self.W[i]["o"], wo[slot * D:(slot + 1) * D, :], D, D)
            cast(self.W[i]["w1"], I["mlp_w1"][i * D:(i + 1) * D, :], D, HID)
            for j in range(HID // D):
                cast(self.W[i]["w2"][j], I["mlp_w2"][i * HID + j * D:i * HID + (j + 1) * D, :], D, D)
            cast(self.W[i]["md"], I["mod_down"][i * D:(i + 1) * D, :], D, RANK)
            cast(self.W[i]["mu"], I["mod_up"][i * RANK:(i + 1) * RANK, :], RANK, 6 * D)

    def phase_silu(self):
        cfg, tr, nc, I = self.cfg, self.tr, self.nc, self.I
        KC, NG, B = cfg.KC, cfg.NG, cfg.B
        with ExitStack() as es:
            cT = self.sb(es, [128, NG, KC], F32, "cT", slot=True)
            pairs = []
            for g in range(NG):
                src = I["c"][g, :] if g < B else I["c_ctx"][0, :]
                pairs += self.col_pairs(cT.t[:, g, :], src, KC)
            tr.dma(tr.sp, cT.res, pairs, writes=[cT.res], allow_slow_non_contiguous=True)
            for g in range(NG):
                tr.op(tr.act, lambda g=g: nc.scalar.activation(out=self.sT.t[:, :, g], in_=cT.t[:, g, :], func=AF.Silu),
                      reads=[cT.res], writes=[self.sT.res])
            tr.barrier()

    def tok_src(self, b, j0, n):
        cfg, I = self.cfg, self.I
        if j0 < cfg.CTX:
            return I["ctx"][b * cfg.CTX + j0:b * cfg.CTX + j0 + n, :]
        s0 = j0 - cfg.CTX
        return I["x"][b * cfg.S + s0:b * cfg.S + s0 + n, :]

    def phase_in_transpose(self):
        cfg, tr, nc = self.cfg, self.tr, self.nc
        D, KC, NB, B = cfg.D, cfg.KC, cfg.NB, cfg.B
        with ExitStack() as es:
            xin = [self.sb(es, [128, 2, D], F32, "xin", slot=True) for _ in range(2)]
            stg = [self.sb(es, [128, 256], F32, "tstg", slot=True) for _ in range(4)]
            pss = [self.ps(es, "tps") for _ in range(4)]
            xr, sr, pr = Ring(xin), Ring(stg), Ring(pss)
            k = 0
            for b in range(B):
                for j0 in range(0, NB, 256):
                    xb = xr.next()
                    src = self.tok_src(b, j0, 256).rearrange("(j p) d -> p j d", p=128)
                    tr.dma(tr.sp, xb.res, [(xb.t[:], src)], writes=[xb.res])
                    t0 = b * NB + j0
                    for c in range(KC):
                        pb = pr.next()
                        tr.op(tr.pe, lambda: [nc.tensor.transpose(out=pb.t[:, j * 128:(j + 1) * 128],
                                                                  in_=xb.t[:, j, c * 128:(c + 1) * 128],
                                                                  identity=self.ident.t[:]) for j in range(2)],
                              reads=[xb.res, self.ident.res], writes=[pb.res])
                        sb_ = sr.next()
                        if k % 2 == 0:
                            tr.op(tr.act, lambda: nc.scalar.copy(out=sb_.t[:], in_=pb.t[:, 0:256]),
                                  reads=[pb.res], writes=[sb_.res])
                        else:
                            tr.op(tr.dve, lambda: nc.vector.tensor_copy(out=sb_.t[:], in_=pb.t[:, 0:256]),
                                  reads=[pb.res], writes=[sb_.res])
                        k += 1
                        tr.dma(tr.pool, sb_.res, [(self.hT[c * 128:(c + 1) * 128, t0:t0 + 256], sb_.t[:])],
                               reads=[sb_.res])
            tr.barrier()

    def groups_in(self, t0, n):
        cfg = self.cfg
        res = []
        t = t0
        while t < t0 + n:
            b, j = divmod(t, cfg.NB)
            if j < cfg.CTX:
                e = min(t0 + n, b * cfg.NB + cfg.CTX)
                g = cfg.B
            else:
                e = min(t0 + n, (b + 1) * cfg.NB)
                g = b
            res.append((t - t0, e - t, g))
            t = e
        return res

    def phase_mod(self, i):
        cfg, tr, nc, I = self.cfg, self.tr, self.nc, self.I
        D, KC, NG, RANK = cfg.D, cfg.KC, cfg.NG, cfg.RANK
        RC = RANK // 128
        Wd, Wu = self.W[i]["md"], self.W[i]["mu"]
        with ExitStack() as es:
            wds = [self.sb(es, [128, KC * 256], BF16, "wd", slot=True) for _ in range(2)]
            MU = 4
            wus = [self.sb(es, [128, MU, RC * 256], BF16, "wu", slot=True) for _ in range(2)]
            rT = self.sb(es, [128, RC, NG], BF16, "rT")
            pss = [self.ps(es, "mps") for _ in range(2)]
            wdr, wur, pr = Ring(wds), Ring(wus), Ring(pss)
            tr.dma(tr.sp, self.n1g.res, self.col_pairs(self.n1g.t, I["norm1_g"][i, :], KC),
                   writes=[self.n1g.res], allow_slow_non_contiguous=True)
            tr.dma(tr.sp, self.n2g.res, self.col_pairs(self.n2g.t, I["norm2_g"][i, :], KC),
                   writes=[self.n2g.res], allow_slow_non_contiguous=True)
            tr.dma(tr.sp, self.mbT.res, self.col_pairs(self.mbT.t, I["mod_b"][i, :], 6 * KC),
                   writes=[self.mbT.res], allow_slow_non_contiguous=True)
            for rp in range(RC // 2):
                wb = wdr.next()
                tr.dma(tr.sp, wb.res, [(wb.t[:], Wd[rp, :, :])], writes=[wb.res])
                w3 = wb.t[:].rearrange("p (c n) -> p c n", n=256)
                for h2 in range(2):
                    pb = pr.next()
                    tr.op(tr.pe, lambda: [nc.tensor.matmul(pb.t[:, 0:NG], w3[:, k, h2 * 128:(h2 + 1) * 128],
                                                           self.sT.t[:, k, :], start=(k == 0), stop=(k == KC - 1))
                                          for k in range(KC)],
                          reads=[wb.res, self.sT.res], writes=[pb.res])
                    tr.op(tr.dve, lambda: nc.vector.tensor_copy(out=rT.t[:, rp * 2 + h2, :], in_=pb.t[:, 0:NG]),
                          reads=[pb.res], writes=[rT.res])
            npair = 6 * D // 256
            for m0 in range(0, npair, MU):
                wb = wur.next()
                mu = min(MU, npair - m0)
                tr.dma(tr.sp, wb.res, [(wb.t[:, 0:mu, :], Wu[m0:m0 + mu, :, :].rearrange("m p x -> p m x"))],
                       writes=[wb.res])
                for mi in range(mu):
                    w3 = wb.t[:, mi, :].rearrange("p (c n) -> p c n", n=256)
                    for h2 in range(2):
                        mc = (m0 + mi) * 2 + h2
                        pb = pr.next()
                        tr.op(tr.pe, lambda: [nc.tensor.matmul(pb.t[:, 0:NG], w3[:, k, h2 * 128:(h2 + 1) * 128],
                                                               rT.t[:, k, :], start=(k == 0), stop=(k == RC - 1))
                                              for k in range(RC)],
                              reads=[wb.res, rT.res], writes=[pb.res])
                        tr.op(tr.dve, lambda: nc.vector.tensor_scalar(out=self.modT.t[:, mc, :], in0=pb.t[:, 0:NG],
                                                                      scalar1=self.mbT.t[:, mc:mc + 1], scalar2=None,
                                                                      op0=ALU.add),
                              reads=[pb.res, self.mbT.res], writes=[self.modT.res])
            for (gs, ng, mi) in ((self.gs1, self.n1g, 1), (self.gs2, self.n2g, 4)):
                for g in range(NG):
                    tr.op(tr.dve, lambda: nc.vector.scalar_tensor_tensor(
                        out=gs.t[:, :, g], in0=self.modT.t[:, mi * KC:(mi + 1) * KC, g], scalar=1.0, in1=ng.t[:],
                        op0=ALU.add, op1=ALU.mult), reads=[self.modT.res, ng.res], writes=[gs.res])
            tr.barrier()

    def mod_col(self, mi, c, g):
        return self.modT.t[:, mi * self.cfg.KC + c, g:g + 1]

    def rstd_from_psum(self, out_ap, ps_ap, n, reads, writes):
        tr, nc = self.tr, self.nc
        tr.op(tr.act, lambda: nc.scalar.activation(out=out_ap, in_=ps_ap, func=AF.Sqrt, bias=self.epsT.t[:, 0:1],
                                                   scale=1.0 / n), reads=list(reads) + [self.epsT.res], writes=writes)
        tr.op(tr.dve, lambda: nc.vector.reciprocal(out=out_ap, in_=out_ap), reads=writes, writes=writes)

    def phase_norm(self, gs, shift_mi, dst):
        cfg, tr, nc = self.cfg, self.tr, self.nc
        D, KC, NTOK = cfg.D, cfg.KC, cfg.NTOK
        TW = 256
        with ExitStack() as es:
            hin = [self.sb(es, [128, KC, TW], F32, "hin", slot=True) for _ in range(2)]
            sq = [self.sb(es, [128, KC, TW], BF16, "sq") for _ in range(2)]
            tmp = [self.sb(es, [128, TW], F32, "ntmp") for _ in range(4)]
            uo = [self.sb(es, [128, KC, TW], BF16, "uo", slot=True) for _ in range(2)]
            rs = [self.sb(es, [128, TW], F32, "rstd") for _ in range(2)]
            pss = [self.ps(es, "nps") for _ in range(2)]
            hr, sqr, tmr, uor, rsr, pr = Ring(hin), Ring(sq), Ring(tmp), Ring(uo), Ring(rs), Ring(pss)
            for t0 in range(0, NTOK, TW):
                (_, _, g), = self.groups_in(t0, TW)
                hb, sb_, ub, rb, pb = hr.next(), sqr.next(), uor.next(), rsr.next(), pr.next()
                tr.dma(tr.sp, hb.res, self.chunk_pairs(hb.t, self.hT, t0, TW, True), writes=[hb.res])
                tr.op(tr.act, lambda: nc.scalar.activation(out=sb_.t[:], in_=hb.t[:], func=AF.Square),
                      reads=[hb.res], writes=[sb_.res])
                tr.op(tr.pe, lambda: [nc.tensor.matmul(pb.t[:, 0:TW], self.onesb.t[:], sb_.t[:, k, :],
                                                       start=(k == 0), stop=(k == KC - 1)) for k in range(KC)],
                      reads=[sb_.res, self.onesb.res], writes=[pb.res])
                self.rstd_from_psum(rb.t[:], pb.t[:, 0:TW], D, [pb.res], [rb.res])
                for c in range(KC):
                    tb = tmr.next()
                    tr.op(tr.dve, lambda: nc.vector.scalar_tensor_tensor(
                        out=tb.t[:], in0=hb.t[:, c, :], scalar=gs.t[:, c, g:g + 1], in1=rb.t[:],
                        op0=ALU.mult, op1=ALU.mult), reads=[hb.res, rb.res, gs.res], writes=[tb.res])
                    tr.op(tr.act, lambda: nc.scalar.activation(out=ub.t[:, c, :], in_=tb.t[:], func=AF.Identity,
                                                               bias=self.mod_col(shift_mi, c, g), scale=1.0),
                          reads=[tb.res, self.modT.res], writes=[ub.res])
                tr.dma(tr.pool, ub.res, self.chunk_pairs(ub.t, dst, t0, TW, False), reads=[ub.res])
            tr.barrier()

    def gemm(self, X, Wt, K, N, epi, epi_alloc=None):
        cfg, tr, nc = self.cfg, self.tr, self.nc
        NTOK = cfg.NTOK
        kc = K // 128
        XG = min(8, kc)
        ngrp = kc // XG
        TT = 1024
        with ExitStack() as es:
            nring = ngrp + max(1, ngrp // 4)
            xs = [self.sb(es, [128, XG, TT], BF16, "xs", slot=True) for _ in range(nring)]
            ws = [self.sb(es, [128, kc * 256], BF16, "ws", slot=True) for _ in range(3)]
            psets = [[self.ps(es, "gps") for _ in range(2)] for _ in range(3)]
            xr, wr, pr = Ring(xs), Ring(ws), Ring(psets)
            st = epi_alloc(es) if epi_alloc else None
            for t0 in range(0, NTOK, TT):
                tw = min(TT, NTOK - t0)
                nh = tw // 512
                xg = []
                for gi in range(ngrp):
                    xb = xr.next()
                    tr.dma(tr.sp, xb.res, [(xb.t[:, :, 0:tw],
                                            X[gi * XG * 128:(gi + 1) * XG * 128, t0:t0 + tw].rearrange(
                                                "(c p) t -> p c t", p=128))], writes=[xb.res])
                    xg.append(xb)
                for npair in range(N // 256):
                    wb = wr.next()
                    tr.dma(tr.sp, wb.res, [(wb.t[:], Wt[npair, :, :])], writes=[wb.res])
                    w3 = wb.t[:].rearrange("p (c n) -> p c n", n=256)
                    for h2 in range(2):
                        nch = npair * 2 + h2
                        banks = pr.next()
                        for gi in range(ngrp):
                            xb = xg[gi]
                            tr.op(tr.pe, lambda: [nc.tensor.matmul(banks[hf].t[:, 0:512],
                                                                   w3[:, gi * XG + k, h2 * 128:(h2 + 1) * 128],
                                                                   xb.t[:, k, hf * 512:(hf + 1) * 512],
                                                                   start=(gi == 0 and k == 0),
                                                                   stop=(gi == ngrp - 1 and k == XG - 1))
                                                  for k in range(XG) for hf in range(nh)],
                                  reads=[wb.res, xb.res], writes=[banks[hf].res for hf in range(nh)])
                        epi(st, nch, t0, banks[:nh])
            tr.barrier()

    def epi_store_alloc(self, es):
        st = {"stg": Ring([self.sb(es, [128, 1024], BF16, "stg", slot=True) for _ in range(3)]), "k": 0}
        return st

    def evac(self, st, out_ap, in_ap, reads, writes):
        tr, nc = self.tr, self.nc
        st["k"] += 1
        import os
        if os.environ.get("KEVAC") == "dve":
            st["k"] = 1
        if os.environ.get("KEVAC") == "act":
            st["k"] = 0
        if st["k"] % 2 == 0:
            tr.op(tr.act, lambda: nc.scalar.copy(out=out_ap, in_=in_ap), reads=reads, writes=writes)
        else:
            tr.op(tr.dve, lambda: nc.vector.tensor_copy(out=out_ap, in_=in_ap), reads=reads, writes=writes)

    def make_epi_plain(self, dst, row_of=lambda nch: nch * 128):
        tr = self.tr

        def epi(st, nch, t0, banks):
            sg = st["stg"].next()
            for hf, bk in enumerate(banks):
                self.evac(st, sg.t[:, hf * 512:(hf + 1) * 512], bk.t[:, 0:512], [bk.res], [sg.res])
            tw = 512 * len(banks)
            r0 = row_of(nch)
            tr.dma(tr.pool, sg.res, [(dst[r0:r0 + 128, t0:t0 + tw], sg.t[:, 0:tw])], reads=[sg.res])
        return epi

    def make_epi_relu2(self, dst):
        tr, nc = self.tr, self.nc

        def alloc(es):
            st = self.epi_store_alloc(es)
            st["sq"] = Ring([self.sb(es, [128, 512], F32, "r2") for _ in range(3)])
            return st

        def epi(st, nch, t0, banks):
            sg = st["stg"].next()
            for hf, bk in enumerate(banks):
                s2 = st["sq"].next()
                tr.op(tr.act, lambda: nc.scalar.activation(out=s2.t[:], in_=bk.t[:, 0:512], func=AF.Square),
                      reads=[bk.res], writes=[s2.res])
                tr.op(tr.dve, lambda: nc.vector.scalar_tensor_tensor(
                    out=sg.t[:, hf * 512:(hf + 1) * 512], in0=bk.t[:, 0:512], scalar=0.0, in1=s2.t[:],
                    op0=ALU.is_gt, op1=ALU.mult), reads=[bk.res, s2.res], writes=[sg.res])
            tw = 512 * len(banks)
            tr.dma(tr.pool, sg.res, [(dst[nch * 128:(nch + 1) * 128, t0:t0 + tw], sg.t[:, 0:tw])], reads=[sg.res])
        return alloc, epi

    def make_epi_resid(self, gate_mi):
        tr, nc = self.tr, self.nc

        def alloc(es):
            return {"h": Ring([self.sb(es, [128, 1024], F32, "hres", slot=True) for _ in range(3)])}

        def epi(st, nch, t0, banks):
            hb = st["h"].next()
            tw = 512 * len(banks)
            dr = self.hT[nch * 128:(nch + 1) * 128, t0:t0 + tw]
            tr.dma(tr.sp, hb.res, [(hb.t[:, 0:tw], dr)], writes=[hb.res])
            for hf, bk in enumerate(banks):
                for (o, n, g) in self.groups_in(t0 + hf * 512, 512):
                    a, b_ = hf * 512 + o, hf * 512 + o + n
                    tr.op(tr.dve, lambda: nc.vector.scalar_tensor_tensor(
                        out=hb.t[:, a:b_], in0=bk.t[:, o:o + n], scalar=self.mod_col(gate_mi, nch, g),
                        in1=hb.t[:, a:b_], op0=ALU.mult, op1=ALU.add),
                        reads=[bk.res, self.modT.res, hb.res], writes=[hb.res])
            tr.dma(tr.pool, hb.res, [(dr, hb.t[:, 0:tw])], reads=[hb.res])
        return alloc, epi

    def make_epi_qkv(self, kind, slot):
        cfg, tr, nc, I = self.cfg, self.tr, self.nc, self.I
        D = cfg.D
        plain = self.make_epi_plain(self.qkvT)
        if kind == 1:
            return self.epi_store_alloc, plain
        nq = cfg.AH if kind == 0 else D // 128
        nk = cfg.AKV if kind == 0 else D // 128

        def alloc(es):
            st = self.epi_store_alloc(es)
            st["cos"] = Ring([self.sb(es, [128, 1024], F32, "cos", slot=True) for _ in range(2)])
            st["sin"] = Ring([self.sb(es, [128, 1024], F32, "sin", slot=True) for _ in range(2)])
            st["t0"] = None
            st["qg"] = Ring([self.sb(es, [128, 512], BF16, "qg") for _ in range(2)])
            st["t1"] = Ring([self.sb(es, [128, 512], F32, "t1") for _ in range(2)])
            st["t2"] = Ring([self.sb(es, [128, 512], F32, "t2") for _ in range(2)])
            st["ps"] = Ring([self.ps(es, "eps") for _ in range(2)])
            if kind == 0:
                st["sq"] = Ring([self.sb(es, [128, 512], BF16, "esq") for _ in range(2)])
                st["rs"] = Ring([self.sb(es, [128, 512], F32, "ers") for _ in range(2)])
                st["gain"] = self.sb(es, [128, 2], F32, "gain", slot=True)
                tr.dma(tr.sp, st["gain"].res,
                       [(st["gain"].t[:, 0:1], I["a_q_g"][slot, :].rearrange("(p o) -> p o", o=1)),
                        (st["gain"].t[:, 1:2], I["a_k_g"][slot, :].rearrange("(p o) -> p o", o=1))],
                       writes=[st["gain"].res], allow_slow_non_contiguous=True)
            return st

        def epi(st, nch, t0, banks):
            if nch >= nq + nk:
                return plain(st, nch, t0, banks)
            tw = 512 * len(banks)
            if st["t0"] != t0:
                st["t0"] = t0
                st["cb"], st["sb"] = st["cos"].next(), st["sin"].next()
                tr.dma(tr.sp, st["cb"].res, [(st["cb"].t[:, 0:tw], I["k_cos"][:, t0:t0 + tw])], writes=[st["cb"].res])
                tr.dma(tr.sp, st["sb"].res, [(st["sb"].t[:, 0:tw], I["k_sin"][:, t0:t0 + tw])], writes=[st["sb"].res])
            cb, sb_ = st["cb"], st["sb"]
            sg = st["stg"].next()
            for hf, bk in enumerate(banks):
                qg, t1, t2, pr = st["qg"].next(), st["t1"].next(), st["t2"].next(), st["ps"].next()
                cs = slice(hf * 512, (hf + 1) * 512)
                if kind == 0:
                    sq, rs = st["sq"].next(), st["rs"].next()
                    gcol = st["gain"].t[:, 0:1] if nch < nq else st["gain"].t[:, 1:2]
                    tr.op(tr.act, lambda: nc.scalar.activation(out=sq.t[:], in_=bk.t[:, 0:512], func=AF.Square),
                          reads=[bk.res], writes=[sq.res])
                    tr.op(tr.act, lambda: nc.scalar.activation(out=qg.t[:], in_=bk.t[:, 0:512], func=AF.Copy,
                                                               scale=gcol),
                          reads=[bk.res, st["gain"].res], writes=[qg.res])
                    tr.op(tr.pe, lambda: nc.tensor.matmul(pr.t[:, 0:512], self.onesb.t[:], sq.t[:], start=True, stop=True),
                          reads=[sq.res, self.onesb.res], writes=[pr.res])
                    self.rstd_from_psum(rs.t[:], pr.t[:, 0:512], 128, [pr.res], [rs.res])
                else:
                    self.evac(st, qg.t[:], bk.t[:, 0:512], [bk.res], [qg.res])
                tr.op(tr.pe, lambda: nc.tensor.matmul(pr.t[:, 0:512], self.swapb.t[:], qg.t[:], start=True, stop=True),
                      reads=[qg.res, self.swapb.res, pr.res], writes=[pr.res])
                tr.op(tr.dve, lambda: nc.vector.tensor_tensor(out=t1.t[:], in0=qg.t[:], in1=cb.t[:, cs], op=ALU.mult),
                      reads=[qg.res, cb.res], writes=[t1.res])
                tr.op(tr.dve, lambda: nc.vector.tensor_tensor(out=t2.t[:], in0=pr.t[:, 0:512], in1=sb_.t[:, cs], op=ALU.mult),
                      reads=[pr.res, sb_.res], writes=[t2.res])
                if kind == 0:
                    tr.op(tr.pool, lambda: nc.gpsimd.tensor_tensor(out=t1.t[:], in0=t1.t[:], in1=t2.t[:], op=ALU.add),
                          reads=[t1.res, t2.res], writes=[t1.res])
                    tr.op(tr.dve, lambda: nc.vector.tensor_tensor(out=sg.t[:, cs], in0=t1.t[:], in1=rs.t[:], op=ALU.mult),
                          reads=[t1.res, rs.res], writes=[sg.res])
                else:
                    tr.op(tr.pool, lambda: nc.gpsimd.tensor_tensor(out=sg.t[:, cs], in0=t1.t[:], in1=t2.t[:], op=ALU.add),
                          reads=[t1.res, t2.res], writes=[sg.res])
            tr.dma(tr.pool, sg.res, [(self.qkvT[nch * 128:(nch + 1) * 128, t0:t0 + tw], sg.t[:, 0:tw])], reads=[sg.res])
        return alloc, epi

    def load_vT_transposed(self, st, vdst, row0, tok0, nblk, ncol=1, col0=0):
        tr, nc = self.tr, self.nc
        vt = st["vt"].next()
        tr.dma(tr.sp, vt.res, [(vt.t[:, 0:nblk * 128], self.qkvT[row0:row0 + 128, tok0:tok0 + nblk * 128])],
               writes=[vt.res])
        for b0 in range(0, nblk, 8):
            nb_ = min(8, nblk - b0)
            pb = st["tps"].next()
            tr.op(tr.pe, lambda: [nc.tensor.transpose(out=pb.t[:, j * 128:(j + 1) * 128],
                                                      in_=vt.t[:, (b0 + j) * 128:(b0 + j + 1) * 128],
                                                      identity=self.identb.t[:]) for j in range(nb_)],
                  reads=[vt.res, self.identb.res], writes=[pb.res])
            tr.op(tr.dve, lambda: nc.vector.tensor_copy(
                out=vdst.t[:, b0:b0 + nb_, col0 * 128:(col0 + 1) * 128],
                in_=pb.t[:, 0:nb_ * 128].rearrange("p (j d) -> p j d", d=128)),
                reads=[pb.res], writes=[vdst.res])

    def attn_common_alloc(self, es, nbk, vcols=128):
        st = {}
        st["kT"] = Ring([self.sb(es, [128, nbk * 128], BF16, "kT", slot=True) for _ in range(2)])
        st["vt"] = Ring([self.sb(es, [128, nbk * 128], BF16, "vt", slot=True) for _ in range(2)])
        st["v"] = Ring([self.sb(es, [128, nbk, vcols], BF16, "v") for _ in range(2)])
        st["tps"] = Ring([self.ps(es, "tpsb", dt=BF16, cols=1024)])
        st["ostg"] = Ring([self.sb(es, [128, 512], BF16, "ostg", slot=True) for _ in range(3)])
        st["rden"] = Ring([self.sb(es, [128, 512], F32, "rden") for _ in range(2)])
        return st

    def attend_block(self, st, qT_ap, qres, nq, kT, v, ktiles, sc_ring, p_ring, o_bank, d_bank, scale, exp_fn=None):
        tr, nc = self.tr, self.nc
        n = len(ktiles)
        pend = []

        def issue_pv(idx, pb):
            kt = ktiles[idx]
            tr.op(tr.pe, lambda: [nc.tensor.matmul(o_bank.t[:, 0:nq], v.t[:, kt, 0:128], pb.t[:, 0:nq],
                                                   start=(idx == 0), stop=(idx == n - 1)),
                                  nc.tensor.matmul(d_bank.t[:, 0:nq], self.onesb.t[:], pb.t[:, 0:nq],
                                                   start=(idx == 0), stop=(idx == n - 1))],
                  reads=[v.res, pb.res, self.onesb.res], writes=[o_bank.res, d_bank.res])

        for idx, kt in enumerate(ktiles):
            sb_ = sc_ring.next()
            tr.op(tr.pe, lambda: nc.tensor.matmul(sb_.t[:, 0:nq], kT.t[:, kt * 128:(kt + 1) * 128], qT_ap,
                                                  start=True, stop=True),
                  reads=[kT.res, qres], writes=[sb_.res])
            pb = p_ring.next()
            if exp_fn is not None and exp_fn(idx, kt, sb_, pb):
                pass
            else:
                tr.op(tr.act, lambda: nc.scalar.activation(out=pb.t[:, 0:nq], in_=sb_.t[:, 0:nq], func=AF.Exp, scale=scale),
                      reads=[sb_.res], writes=[pb.res])
            pend.append((idx, pb))
            if len(pend) > 1:
                issue_pv(*pend.pop(0))
        while pend:
            issue_pv(*pend.pop(0))

    def finish_o(self, st, o_bank, d_bank, nq, dst_rows, tok0):
        tr, nc = self.tr, self.nc
        rd, og = st["rden"].next(), st["ostg"].next()
        tr.op(tr.dve, lambda: nc.vector.reciprocal(out=rd.t[:, 0:nq], in_=d_bank.t[:, 0:nq]),
              reads=[d_bank.res], writes=[rd.res])
        tr.op(tr.dve, lambda: nc.vector.tensor_tensor(out=og.t[:, 0:nq], in0=o_bank.t[:, 0:nq], in1=rd.t[:, 0:nq],
                                                      op=ALU.mult), reads=[o_bank.res, rd.res], writes=[og.res])
        tr.dma(tr.pool, og.res, [(self.attnT[dst_rows:dst_rows + 128, tok0:tok0 + nq], og.t[:, 0:nq])], reads=[og.res])

    def phase_attn_gqa(self, with_ctx):
        cfg, tr, nc = self.cfg, self.tr, self.nc
        B, S, CTX, NB, D = cfg.B, cfg.S, cfg.CTX, cfg.NB, cfg.D
        nbk = NB // 128
        grp = cfg.AH // cfg.AKV
        scale = 128 ** -0.5
        kr0 = cfg.AH * 128
        vr0 = kr0 + cfg.AKV * 128
        with ExitStack() as es:
            st = self.attn_common_alloc(es, nbk)
            qTs = Ring([self.sb(es, [128, NB], BF16, "qT", slot=True) for _ in range(2)])
            sc = Ring([self.ps(es, "sc") for _ in range(3)])
            pr = Ring([self.sb(es, [128, 512], BF16, "pT") for _ in range(4)])
            ob = Ring([self.ps(es, "ob") for _ in range(2)])
            db = Ring([self.ps(es, "db") for _ in range(2)])
            for b in range(B):
                tb = b * NB
                for kv in range(cfg.AKV):
                    kT, v = st["kT"].next(), st["v"].next()
                    tr.dma(tr.sp, kT.res, [(kT.t[:], self.qkvT[kr0 + kv * 128:kr0 + (kv + 1) * 128, tb:tb + NB])],
                           writes=[kT.res])
                    self.load_vT_transposed(st, v, vr0 + kv * 128, tb, nbk)
                    for g in range(grp):
                        h = kv * grp + g
                        qT = qTs.next()
                        tr.dma(tr.sp, qT.res, [(qT.t[:], self.qkvT[h * 128:(h + 1) * 128, tb:tb + NB])], writes=[qT.res])
                        tiles = [(CTX + q0, 512, list(range(nbk))) for q0 in range(0, S, 512)]
                        if with_ctx:
                            tiles.append((0, CTX, list(range(CTX // 128))))
                        for (q0, nq, kts) in tiles:
                            o_b, d_b = ob.next(), db.next()
                            self.attend_block(st, qT.t[:, q0:q0 + nq], qT.res, nq, kT, v, kts, sc, pr, o_b, d_b, scale)
                            self.finish_o(st, o_b, d_b, nq, h * 128, tb + q0)
            tr.barrier()

    def phase_attn_diff(self, slot, lambda_init, with_ctx):
        cfg, tr, nc, I = self.cfg, self.tr, self.nc, self.I
        B, S, CTX, NB, D = cfg.B, cfg.S, cfg.CTX, cfg.NB, cfg.D
        nbk = NB // 128
        scale = 128 ** -0.5
        with ExitStack() as es:
            st = self.attn_common_alloc(es, nbk, vcols=256)
            kT2 = Ring([self.sb(es, [128, nbk * 128], BF16, "kT2", slot=True) for _ in range(2)])
            qTs = Ring([self.sb(es, [128, 2, NB], BF16, "qTd", slot=True) for _ in range(2)])
            sc = Ring([self.ps(es, "sc") for _ in range(1)])
            pr = Ring([self.sb(es, [128, 512], BF16, "pT") for _ in range(4)])
            obs = [[self.ps(es, "ob") for _ in range(2)] for _ in range(2)]
            dbs = [self.ps(es, "db") for _ in range(2)]
            lam4 = self.sb(es, [128, 4], F32, "lam4", slot=True)
            lamw = self.sb(es, [128, 4], F32, "lamw")
            nlam = self.sb(es, [128, 1], F32, "nlam")
            sub = self.sb(es, [128, 2], F32, "subg", slot=True)
            tA = Ring([self.sb(es, [128, 512], F32, "tA") for _ in range(4)])
            oc = Ring([self.sb(es, [128, 2, 512], F32, "oc") for _ in range(2)])
            osq = Ring([self.sb(es, [128, 2, 512], BF16, "osq") for _ in range(2)])
            rs = Ring([self.sb(es, [128, 512], F32, "drs") for _ in range(2)])
            tr.dma(tr.sp, lam4.res, [(lam4.t[:, j:j + 1], I["c_lam"][slot * 4 + j, :].rearrange("(p o) -> p o", o=1))
                                     for j in range(4)], writes=[lam4.res], allow_slow_non_contiguous=True)
            tr.dma(tr.sp, sub.res, [(sub.t[:, j:j + 1], I["c_subln_g"][slot, j * 128:(j + 1) * 128].rearrange("(p o) -> p o", o=1))
                                    for j in range(2)], writes=[sub.res], allow_slow_non_contiguous=True)
            tr.op(tr.dve, lambda: nc.vector.tensor_tensor(out=lamw.t[:, 0:1], in0=lam4.t[:, 0:1], in1=lam4.t[:, 1:2], op=ALU.mult),
                  reads=[lam4.res], writes=[lamw.res])
            tr.op(tr.dve, lambda: nc.vector.tensor_tensor(out=lamw.t[:, 1:2], in0=lam4.t[:, 2:3], in1=lam4.t[:, 3:4], op=ALU.mult),
                  reads=[lam4.res, lamw.res], writes=[lamw.res])
            lamb = self.sb(es, [128, 2], BF16, "lamb")
            onesf = self.sb(es, [128, 128], F32, "onesf")
            tr.op(tr.dve, lambda: nc.vector.memset(onesf.t[:], 1.0), writes=[onesf.res])
            d0 = dbs[0]
            tr.op(tr.pe, lambda: nc.tensor.matmul(d0.t[:, 0:2], onesf.t[:], lamw.t[:, 0:2], start=True, stop=True),
                  reads=[onesf.res, lamw.res], writes=[d0.res])
            tr.op(tr.act, lambda: nc.scalar.activation(out=lamw.t[:, 2:4], in_=d0.t[:, 0:2], func=AF.Exp),
                  reads=[d0.res, lamw.res], writes=[lamw.res])
            tr.op(tr.dve, lambda: nc.vector.tensor_tensor(out=nlam.t[:], in0=lamw.t[:, 3:4], in1=lamw.t[:, 2:3], op=ALU.subtract),
                  reads=[lamw.res], writes=[nlam.res])
            tr.op(tr.dve, lambda: nc.vector.tensor_scalar(out=nlam.t[:], in0=nlam.t[:], scalar1=-float(lambda_init),
                                                          scalar2=None, op0=ALU.add), reads=[nlam.res], writes=[nlam.res])
            tr.op(tr.dve, lambda: nc.vector.tensor_scalar(out=sub.t[:], in0=sub.t[:], scalar1=float(1.0 - lambda_init),
                                                          scalar2=None, op0=ALU.mult), reads=[sub.res], writes=[sub.res])
            for b in range(B):
                tb = b * NB
                for h in range(cfg.CH):
                    kTa, kTb, v, qT = st["kT"].next(), kT2.next(), st["v"].next(), qTs.next()
                    r = h * 256
                    tr.dma(tr.sp, kTa.res, [(kTa.t[:], self.qkvT[D + r:D + r + 128, tb:tb + NB])], writes=[kTa.res])
                    tr.dma(tr.sp, kTb.res, [(kTb.t[:], self.qkvT[D + r + 128:D + r + 256, tb:tb + NB])], writes=[kTb.res])
                    tr.dma(tr.sp, qT.res, [(qT.t[:, 0, :], self.qkvT[r:r + 128, tb:tb + NB]),
                                           (qT.t[:, 1, :], self.qkvT[r + 128:r + 256, tb:tb + NB])], writes=[qT.res])
                    for j in range(2):
                        self.load_vT_transposed(st, v, 2 * D + r + j * 128, tb, nbk, col0=j)
                    kTs = [kTa, kTb]
                    tiles = [(CTX + q0, 512, list(range(nbk))) for q0 in range(0, S, 512)]
                    if with_ctx:
                        tiles.append((0, CTX, list(range(CTX // 128))))
                    for (q0, nq, kts) in tiles:
                        n = len(kts)
                        for idx, kt in enumerate(kts):
                            for cpt in range(2):
                                sb_ = sc.next()
                                tr.op(tr.pe, lambda: nc.tensor.matmul(sb_.t[:, 0:nq], kTs[cpt].t[:, kt * 128:(kt + 1) * 128],
                                                                      qT.t[:, cpt, q0:q0 + nq], start=True, stop=True),
                                      reads=[kTs[cpt].res, qT.res], writes=[sb_.res])
                                pb = pr.next()
                                tr.op(tr.act, lambda: nc.scalar.activation(out=pb.t[:, 0:nq], in_=sb_.t[:, 0:nq], func=AF.Exp,
                                                                           scale=scale), reads=[sb_.res], writes=[pb.res])
                                tr.op(tr.pe, lambda: [nc.tensor.matmul(obs[cpt][j].t[:, 0:nq], v.t[:, kt, j * 128:(j + 1) * 128],
                                                                       pb.t[:, 0:nq], start=(idx == 0), stop=(idx == n - 1))
                                                      for j in range(2)] +
                                      [nc.tensor.matmul(dbs[cpt].t[:, 0:nq], self.onesb.t[:], pb.t[:, 0:nq],
                                                        start=(idx == 0), stop=(idx == n - 1))],
                                      reads=[v.res, pb.res, self.onesb.res],
                                      writes=[obs[cpt][0].res, obs[cpt][1].res, dbs[cpt].res])
                        rd0, rd1 = tA.next(), tA.next()
                        tr.op(tr.dve, lambda: nc.vector.reciprocal(out=rd0.t[:, 0:nq], in_=dbs[0].t[:, 0:nq]),
                              reads=[dbs[0].res], writes=[rd0.res])
                        tr.op(tr.dve, lambda: nc.vector.reciprocal(out=rd1.t[:, 0:nq], in_=dbs[1].t[:, 0:nq]),
                              reads=[dbs[1].res], writes=[rd1.res])
                        tr.op(tr.dve, lambda: nc.vector.tensor_scalar(out=rd1.t[:, 0:nq], in0=rd1.t[:, 0:nq], scalar1=nlam.t[:, 0:1],
                                                                      scalar2=None, op0=ALU.mult),
                              reads=[rd1.res, nlam.res], writes=[rd1.res])
                        ocb, sqb, rsb = oc.next(), osq.next(), rs.next()
                        for j in range(2):
                            ta = tA.next()
                            tr.op(tr.dve, lambda: nc.vector.tensor_tensor(out=ta.t[:, 0:nq], in0=obs[1][j].t[:, 0:nq], in1=rd1.t[:, 0:nq],
                                                                          op=ALU.mult), reads=[obs[1][j].res, rd1.res], writes=[ta.res])
                            tr.op(tr.dve, lambda: nc.vector.tensor_tensor(out=ocb.t[:, j, 0:nq], in0=obs[0][j].t[:, 0:nq], in1=rd0.t[:, 0:nq],
                                                                          op=ALU.mult), reads=[obs[0][j].res, rd0.res], writes=[ocb.res])
                            tr.op(tr.pool, lambda: nc.gpsimd.tensor_tensor(out=ocb.t[:, j, 0:nq], in0=ocb.t[:, j, 0:nq], in1=ta.t[:, 0:nq],
                                                                           op=ALU.add), reads=[ocb.res, ta.res], writes=[ocb.res])
                            tr.op(tr.act, lambda: nc.scalar.activation(out=sqb.t[:, j, 0:nq], in_=ocb.t[:, j, 0:nq], func=AF.Square),
                                  reads=[ocb.res], writes=[sqb.res])
                        sb_ = sc.next()
                        tr.op(tr.pe, lambda: [nc.tensor.matmul(sb_.t[:, 0:nq], self.onesb.t[:], sqb.t[:, j, 0:nq],
                                                               start=(j == 0), stop=(j == 1)) for j in range(2)],
                              reads=[sqb.res, self.onesb.res], writes=[sb_.res])
                        self.rstd_from_psum(rsb.t[:, 0:nq], sb_.t[:, 0:nq], 256, [sb_.res], [rsb.res])
                        for j in range(2):
                            og = st["ostg"].next()
                            tr.op(tr.dve, lambda: nc.vector.scalar_tensor_tensor(
                                out=og.t[:, 0:nq], in0=ocb.t[:, j, 0:nq], scalar=sub.t[:, j:j + 1], in1=rsb.t[:, 0:nq],
                                op0=ALU.mult, op1=ALU.mult), reads=[ocb.res, sub.res, rsb.res], writes=[og.res])
                            tr.dma(tr.pool, og.res, [(self.attnT[r + j * 128:r + (j + 1) * 128, tb + q0:tb + q0 + nq],
                                                      og.t[:, 0:nq])], reads=[og.res])
            tr.barrier()

    def phase_attn_nbr(self, slot, with_ctx):
        cfg, tr, nc, I = self.cfg, self.tr, self.nc, self.I
        B, S, CTX, NB, D, ROWS = cfg.B, cfg.S, cfg.CTX, cfg.NB, cfg.D, cfg.ROWS
        nbk = NB // 128
        scale = 128 ** -0.5
        wr_ = min(8, ROWS)
        NR = 8
        with ExitStack() as es:
            st = self.attn_common_alloc(es, nbk)
            qTs = Ring([self.sb(es, [128, NB], BF16, "qT", slot=True) for _ in range(2)])
            TB = Ring([self.sb(es, [128, 17, 64], F32, "TB", slot=True) for _ in range(2)])
            cm = self.sb(es, [128, 64], F32, "cm", slot=True)
            rm = self.sb(es, [128, 4], F32, "rm", slot=True)
            sc = Ring([self.ps(es, "sc") for _ in range(3)])
            tmpw = Ring([self.sb(es, [128, 512], F32, "tmpw") for _ in range(3)])
            pr = Ring([self.sb(es, [128, 512], BF16, "pT") for _ in range(4)])
            ob = Ring([self.ps(es, "ob") for _ in range(2)])
            db = Ring([self.ps(es, "db") for _ in range(2)])
            tr.dma(tr.sp, cm.res, [(cm.t[:], I["k_cm"][:, :])], writes=[cm.res])
            tr.dma(tr.sp, rm.res, [(rm.t[:], I["k_rm"][:, :])], writes=[rm.res])
            for h in range(cfg.BH):
                tbt = TB.next()
                base = (slot * cfg.BH + h) * 18
                tr.dma(tr.sp, tbt.res,
                       [(tbt.t[0:64, :, :], I["rpbx"][base:base + 17, :].rearrange("e (k q) -> k e q", q=64)),
                        (tbt.t[64:128, :, :], I["rpbx"][base + 1:base + 18, :].rearrange("e (k q) -> k e q", q=64))],
                       writes=[tbt.res])
                for e in range(17):
                    tr.op(tr.pool, lambda: nc.gpsimd.tensor_tensor(out=tbt.t[:, e, :], in0=tbt.t[:, e, :], in1=cm.t[:], op=ALU.add),
                          reads=[tbt.res, cm.res], writes=[tbt.res])
                for b in range(B):
                    tb = b * NB
                    kT, v, qT = st["kT"].next(), st["v"].next(), qTs.next()
                    tr.dma(tr.sp, kT.res, [(kT.t[:], self.qkvT[D + h * 128:D + (h + 1) * 128, tb:tb + NB])], writes=[kT.res])
                    tr.dma(tr.sp, qT.res, [(qT.t[:], self.qkvT[h * 128:(h + 1) * 128, tb:tb + NB])], writes=[qT.res])
                    self.load_vT_transposed(st, v, 2 * D + h * 128, tb, nbk)
                    for r8 in range(0, ROWS, NR):
                        r0s = [min(max(r - 4, 0), ROWS - wr_) for r in range(r8, r8 + NR)]
                        jmin, jmax = min(r0s) // 2, (max(r0s) + wr_ - 1) // 2
                        kts = [0, 1] + [2 + j for j in range(jmin, jmax + 1)]

                        def exp_fn(idx, kt, sb_, pb):
                            if kt < 2:
                                return False
                            j = kt - 2
                            tw_ = tmpw.next()
                            ri = 0
                            while ri < NR:
                                r, r0 = r8 + ri, r0s[ri]
                                top_ok = r0 <= 2 * j < r0 + wr_
                                bot_ok = r0 <= 2 * j + 1 < r0 + wr_
                                if not (top_ok or bot_ok):
                                    rj = ri
                                    while rj + 1 < NR:
                                        r0n = r0s[rj + 1]
                                        if (r0n <= 2 * j < r0n + wr_) or (r0n <= 2 * j + 1 < r0n + wr_):
                                            break
                                        rj += 1
                                    cs = slice(ri * 64, (rj + 1) * 64)
                                    tr.op(tr.act, lambda: nc.scalar.activation(out=pb.t[:, cs], in_=sb_.t[:, cs], func=AF.Exp,
                                                                               bias=rm.t[:, 3:4], scale=scale),
                                          reads=[sb_.res, rm.res, pb.res], writes=[pb.res])
                                    ri = rj + 1
                                    continue
                                e = 2 * j - r + 8
                                mcol = 0 if (top_ok and bot_ok) else (1 if top_ok else 2)
                                cs = slice(ri * 64, (ri + 1) * 64)
                                tr.op(tr.dve, lambda: nc.vector.scalar_tensor_tensor(
                                    out=tw_.t[:, cs], in0=sb_.t[:, cs], scalar=scale, in1=tbt.t[:, e, :],
                                    op0=ALU.mult, op1=ALU.add), reads=[sb_.res, tbt.res, tw_.res], writes=[tw_.res])
                                tr.op(tr.act, lambda: nc.scalar.activation(out=pb.t[:, cs], in_=tw_.t[:, cs], func=AF.Exp,
                                                                           bias=rm.t[:, mcol:mcol + 1], scale=1.0),
                                      reads=[tw_.res, rm.res, pb.res], writes=[pb.res])
                                ri += 1
                            return True

                        o_b, d_b = ob.next(), db.next()
                        q0 = CTX + r8 * 64
                        self.attend_block(st, qT.t[:, q0:q0 + NR * 64], qT.res, NR * 64, kT, v, kts, sc, pr, o_b, d_b, scale,
                                          exp_fn=exp_fn)
                        self.finish_o(st, o_b, d_b, NR * 64, h * 128, tb + q0)
                    if with_ctx:
                        o_b, d_b = ob.next(), db.next()
                        self.attend_block(st, qT.t[:, 0:CTX], qT.res, CTX, kT, v, list(range(CTX // 128)), sc, pr, o_b, d_b, scale)
                        self.finish_o(st, o_b, d_b, CTX, h * 128, tb)
            tr.barrier()

    def layer_steps(self, i):
        cfg = self.cfg
        D, HID = cfg.D, cfg.HID
        kind, slot = i % 3, i // 3
        with_ctx = i < cfg.DEPTH - 1
        steps = [lambda: self.phase_mod(i), lambda: self.phase_norm(self.gs1, 0, self.uT)]

        def qkv():
            alloc, epi = self.make_epi_qkv(kind, slot)
            self.gemm(self.uT, self.W[i]["qkv"], D, cfg.NQKV[kind], epi, alloc)
        steps.append(qkv)
        if kind == 0:
            steps.append(lambda: self.phase_attn_gqa(with_ctx))
        elif kind == 1:
            steps.append(lambda: self.phase_attn_nbr(slot, with_ctx))
        else:
            steps.append(lambda: self.phase_attn_diff(slot, 0.8 - 0.6 * math.exp(-0.3 * i), with_ctx))

        def oproj():
            alloc, epi = self.make_epi_resid(2)
            self.gemm(self.attnT, self.W[i]["o"], D, D, epi, alloc)
        steps.append(oproj)
        steps.append(lambda: self.phase_norm(self.gs2, 3, self.uT))

        def up():
            alloc, epi = self.make_epi_relu2(self.HT)
            self.gemm(self.uT, self.W[i]["w1"], D, HID, epi, alloc)
        steps.append(up)
        for j in range(HID // D):
            def down(j=j):
                alloc, epi = self.make_epi_resid(5)
                self.gemm(self.HT.aps[j], self.W[i]["w2"][j], D, D, epi, alloc)
            steps.append(down)
        return steps

    def phase_final(self):
        cfg, tr, nc = self.cfg, self.tr, self.nc
        D, KC, NB, B, S, CTX = cfg.D, cfg.KC, cfg.NB, cfg.B, cfg.S, cfg.CTX
        TW = 256
        with ExitStack() as es:
            hin = [self.sb(es, [128, KC, TW], F32, "hin", slot=True) for _ in range(2)]
            sq = [self.sb(es, [128, KC, TW], BF16, "sq") for _ in range(2)]
            yv = [self.sb(es, [128, TW], F32, "yv") for _ in range(4)]
            rs = [self.sb(es, [128, TW], F32, "rstd") for _ in range(2)]
            ot = [self.sb(es, [128, D], F32, "ot", slot=True) for _ in range(2)]
            pss = [self.ps(es, "nps") for _ in range(2)]
            tps = [self.ps(es, "ftp") for _ in range(4)]
            hr, sqr, yr, rsr, otr, pr, tpr = Ring(hin), Ring(sq), Ring(yv), Ring(rs), Ring(ot), Ring(pss), Ring(tps)
            k = 0
            for b in range(B):
                for s0 in range(0, S, TW):
                    t0 = b * NB + CTX + s0
                    hb, sb_, rb, pb = hr.next(), sqr.next(), rsr.next(), pr.next()
                    tr.dma(tr.sp, hb.res, self.chunk_pairs(hb.t, self.hT, t0, TW, True), writes=[hb.res])
                    tr.op(tr.act, lambda: nc.scalar.activation(out=sb_.t[:], in_=hb.t[:], func=AF.Square),
                          reads=[hb.res], writes=[sb_.res])
                    tr.op(tr.pe, lambda: [nc.tensor.matmul(pb.t[:, 0:TW], self.onesb.t[:], sb_.t[:, kk, :],
                                                           start=(kk == 0), stop=(kk == KC - 1)) for kk in range(KC)],
                          reads=[sb_.res, self.onesb.res], writes=[pb.res])
                    self.rstd_from_psum(rb.t[:], pb.t[:, 0:TW], D, [pb.res], [rb.res])
                    obs = [otr.next() for _ in range(TW // 128)]
                    for c0 in range(0, KC, 4):
                        tp = [tpr.next() for _ in range(TW // 128)]
                        for c in range(c0, min(c0 + 4, KC)):
                            yb = yr.next()
                            tr.op(tr.dve, lambda: nc.vector.scalar_tensor_tensor(
                                out=yb.t[:], in0=hb.t[:, c, :], scalar=self.fgT.t[:, c:c + 1], in1=rb.t[:],
                                op0=ALU.mult, op1=ALU.mult), reads=[hb.res, rb.res, self.fgT.res], writes=[yb.res])
                            for j in range(TW // 128):
                                tr.op(tr.pe, lambda: nc.tensor.transpose(out=tp[j].t[:, (c - c0) * 128:(c - c0 + 1) * 128],
                                                                         in_=yb.t[:, j * 128:(j + 1) * 128],
                                                                         identity=self.ident.t[:]),
                                      reads=[yb.res, self.ident.res, tp[j].res], writes=[tp[j].res])
                        ncols = (min(c0 + 4, KC) - c0) * 128
                        for j in range(TW // 128):
                            k += 1
                            if k % 2:
                                tr.op(tr.act, lambda: nc.scalar.copy(out=obs[j].t[:, c0 * 128:c0 * 128 + ncols], in_=tp[j].t[:, 0:ncols]),
                                      reads=[tp[j].res, obs[j].res], writes=[obs[j].res])
                            else:
                                tr.op(tr.dve, lambda: nc.vector.tensor_copy(out=obs[j].t[:, c0 * 128:c0 * 128 + ncols], in_=tp[j].t[:, 0:ncols]),
                                      reads=[tp[j].res, obs[j].res], writes=[obs[j].res])
                    for j in range(TW // 128):
                        r0 = b * S + s0 + j * 128
                        tr.dma(tr.pool, obs[j].res, [(self.out[r0:r0 + 128, :], obs[j].t[:])], reads=[obs[j].res])
            tr.barrier()


def host_constants(cfg):
    NTOK, NB, CTX, S, GW = cfg.NTOK, cfg.NB, cfg.CTX, cfg.S, cfg.GW
    k = {}
    k["k_ident"] = np.eye(128, dtype=np.float32)
    k["k_ones"] = np.ones((128, 128), np.float32)
    sw = np.zeros((128, 128), np.float32)
    for p in range(128):
        sw[p, p ^ 1] = 1.0
    k["k_swap"] = sw
    t = np.arange(S)
    row = (t // GW).astype(np.float32)
    col = (t % GW).astype(np.float32)
    nf = 128 // 4
    inv = (np.float32(10000.0) ** (-np.arange(nf, dtype=np.float32) / np.float32(nf))).astype(np.float32)
    ang = np.concatenate([row[:, None] * inv, col[:, None] * inv], axis=-1).astype(np.float32)
    cos, sin = np.cos(ang).astype(np.float32), np.sin(ang).astype(np.float32)
    cosT = np.ones((128, NB), np.float32)
    sinT = np.zeros((128, NB), np.float32)
    cosT[0::2, CTX:] = cos.T
    cosT[1::2, CTX:] = cos.T
    sinT[0::2, CTX:] = -sin.T
    sinT[1::2, CTX:] = sin.T
    k["k_cos"] = np.ascontiguousarray(np.tile(cosT, (1, cfg.B)))
    k["k_sin"] = np.ascontiguousarray(np.tile(sinT, (1, cfg.B)))
    qc = np.arange(64)
    c0 = np.clip(qc - 8, 0, 64 - 16)
    kc = np.arange(64)
    inw = (kc[:, None] >= c0[None, :]) & (kc[:, None] < c0[None, :] + 16)
    cm = np.where(inw, 0.0, NEG).astype(np.float32)
    k["k_cm"] = np.concatenate([cm, cm], axis=0)
    rm = np.zeros((128, 4), np.float32)
    rm[64:, 1] = NEG
    rm[:64, 2] = NEG
    rm[:, 3] = NEG
    k["k_rm"] = rm
    return k


def host_inputs(cfg, inp):
    D, B, S, CTX, DEPTH = cfg.D, cfg.B, cfg.S, cfg.CTX, cfg.DEPTH
    f = lambda a: np.ascontiguousarray(np.asarray(a, dtype=np.float32))
    m = {}
    m["x"] = f(inp["x"]).reshape(B * S, D)
    m["c"] = f(inp["c"])
    m["ctx"] = f(inp["ctx"]).reshape(B * CTX, D)
    m["c_ctx"] = f(inp["c_ctx"]).reshape(1, D)
    m["norm1_g"] = f(inp["norm1_g"]); m["norm2_g"] = f(inp["norm2_g"])
    m["mod_down"] = f(inp["mod_down"]).reshape(-1, cfg.RANK)
    m["mod_up"] = f(inp["mod_up"]).reshape(-1, 6 * D)
    m["mod_b"] = f(inp["mod_b"])
    m["mlp_w1"] = f(inp["mlp_w1"]).reshape(-1, cfg.HID)
    m["mlp_w2"] = f(inp["mlp_w2"]).reshape(-1, D)
    m["a_w_qkv"] = f(inp["a_w_qkv"]).reshape(-1, cfg.NQKV[0])
    m["a_w_o"] = f(inp["a_w_o"]).reshape(-1, D)
    m["a_q_g"] = f(inp["a_q_g"]); m["a_k_g"] = f(inp["a_k_g"])
    m["b_w_qkv"] = f(inp["b_w_qkv"]).reshape(-1, 3 * D)
    m["b_w_o"] = f(inp["b_w_o"]).reshape(-1, D)
    rpb = f(inp["b_rpb"])
    L, H = rpb.shape[0], rpb.shape[1]
    kc = np.arange(64)[:, None]
    qc = np.arange(64)[None, :]
    dc = np.clip(kc - qc + 15, 0, 30)
    tz = rpb[:, :, :, dc]
    rpbx = np.zeros((L, H, 18, 64, 64), np.float32)
    rpbx[:, :, 1:16] = tz
    m["rpbx"] = rpbx.reshape(L * H * 18, 64 * 64)
    m["c_w_qkv"] = f(inp["c_w_qkv"]).reshape(-1, 3 * D)
    m["c_w_o"] = f(inp["c_w_o"]).reshape(-1, D)
    m["c_lam"] = np.ascontiguousarray(np.stack([f(inp["c_lam_q1"]), f(inp["c_lam_k1"]), f(inp["c_lam_q2"]), f(inp["c_lam_k2"])],
                                               axis=1).reshape(-1, 128))
    m["c_subln_g"] = f(inp["c_subln_g"])
    m["final_g"] = f(inp["final_g"]).reshape(1, D)
    m.update(host_constants(cfg))
    return m


_CACHE = {}
NCORES = 2


def run(cfg_total, inputs, ncores=NCORES):
    import os
    B = cfg_total.B
    assert B % ncores == 0
    bc = B // ncores
    cfg = Cfg(D=cfg_total.D, B=bc, S=cfg_total.S, CTX=cfg_total.CTX, DEPTH=cfg_total.DEPTH, HID=cfg_total.HID,
              RANK=cfg_total.RANK, GW=cfg_total.GW)
    cfg.debug = getattr(cfg_total, "debug", False)
    key = (cfg.D, cfg.B, cfg.S, cfg.HID, cfg.RANK, os.environ.get("KSTOP"), cfg.debug)
    if key not in _CACHE:
        _CACHE[key] = Prog(cfg).build()
    nc = _CACHE[key]
    shared = None
    maps = []
    for c in range(ncores):
        sub = dict(inputs)
        for k in ("x", "c", "ctx"):
            sub[k] = np.asarray(inputs[k])[c * bc:(c + 1) * bc]
        m = host_inputs(cfg, sub)
        if shared is None:
            shared = m
        else:
            for k in m:
                if k not in ("x", "c", "ctx"):
                    m[k] = shared[k]
        maps.append(m)
    res = run_bass_kernel_spmd(nc, maps, core_ids=list(range(ncores)))
    if cfg.debug:
        cfg_total.dbg = res.results[0]
    outs = [np.asarray(res.results[c]["out"]).reshape(bc, cfg.S, cfg.D) for c in range(ncores)]
    return np.concatenate(outs, axis=0)


def kernel(**inputs):
    return run(Cfg(), inputs)
```

```python
import math
from contextlib import ExitStack
import numpy as np
import concourse.bass as bass
import concourse.mybir as mybir
from concourse.bass_utils import run_bass_kernel_spmd

F32 = mybir.dt.float32
BF16 = mybir.dt.bfloat16
ALU = mybir.AluOpType
AF = mybir.ActivationFunctionType
NEG = -30000.0
EPS = 1e-6


class Cfg:
    def __init__(self, D=4096, B=4, S=4096, CTX=256, DEPTH=4, HID=None, RANK=1024, GW=64):
        self.D, self.B, self.S, self.CTX, self.DEPTH, self.RANK, self.GW = D, B, S, CTX, DEPTH, RANK, GW
        self.HID = HID or 4 * D
        self.HD = 128
        self.KC = D // 128
        self.AH = D // 128
        self.AKV = self.AH // 4
        self.BH = D // 128
        self.CH = D // 256
        self.NB = CTX + S
        self.NTOK = B * self.NB
        self.ROWS = S // GW
        self.NQKV = [(self.AH + 2 * self.AKV) * 128, 3 * D, 3 * D]
        self.NQKVMAX = 3 * D
        self.NG = B + 1
        assert self.NTOK % 512 == 0 and S % 512 == 0 and CTX == 256 and GW == 64


class Res:
    __slots__ = ("name", "w", "r")

    def __init__(self, name=""):
        self.name = name
        self.w = None
        self.r = {}


class Slot(Res):
    __slots__ = ("tr", "sems", "cnts")

    def __init__(self, tr, name=""):
        super().__init__(name)
        self.tr = tr
        self.sems = {}
        self.cnts = {}

    def sem_for(self, kind):
        if kind not in self.sems:
            self.sems[kind] = self.tr.new_sem("d%s_%s" % (kind, self.name))
            self.cnts[kind] = 0
        return self.sems[kind]


class Eng:
    def __init__(self, tr, name, h, is_pe=False):
        self.name, self.h, self.is_pe = name, h, is_pe
        self.sem = tr.new_sem("e_" + name)
        self.cnt = 0
        self.known = {}

    def wait(self, ev):
        if ev is None:
            return
        sem, c = ev
        if self.is_pe and sem is self.sem:
            return
        k = id(sem)
        if self.known.get(k, 0) >= c:
            return
        self.h.wait_ge(sem, c)
        self.known[k] = c


class Tracker:
    def __init__(self, nc):
        self.nc = nc
        self.sems = []
        self.pe = Eng(self, "pe", nc.tensor, True)
        self.act = Eng(self, "act", nc.scalar)
        self.dve = Eng(self, "dve", nc.vector)
        self.pool = Eng(self, "pool", nc.gpsimd)
        self.sp = Eng(self, "sp", nc.sync)
        self.engs = [self.pe, self.act, self.dve, self.pool, self.sp]
        self.cur = {}
        self.free_slots = []

    def new_sem(self, name):
        s = self.nc.alloc_semaphore(name=name + "_%d" % len(self.sems))
        self.sems.append(s)
        return s

    def _pre(self, eng, reads, writes):
        for r in reads:
            eng.wait(r.w)
        for w in writes:
            eng.wait(w.w)
            for ev in w.r.values():
                eng.wait(ev)

    def _post(self, ev, reads, writes):
        self.cur[id(ev[0])] = ev
        for r in reads:
            r.r[id(ev[0])] = ev
        for w in writes:
            w.w = ev
            w.r = {}

    def op(self, eng, emit, reads=(), writes=()):
        self._pre(eng, reads, writes)
        inst = emit()
        if isinstance(inst, (list, tuple)):
            inst = inst[-1]
        eng.cnt += 1
        inst.then_inc(eng.sem, 1)
        self._post((eng.sem, eng.cnt), reads, writes)

    def dma(self, q, slot, pairs, reads=(), writes=(), **kw):
        if any(o.dtype != i.dtype for (o, i) in pairs):
            q = self.pool
        self._pre(q, reads, writes)
        kind = "sw" if q is self.pool else "hw"
        sem = slot.sem_for(kind)
        self.ndma = getattr(self, "ndma", {})
        self.ndma[q.name] = self.ndma.get(q.name, 0) + len(pairs)
        for (o, i) in pairs:
            q.h.dma_start(out=o, in_=i, **kw).then_inc(sem, 16)
            slot.cnts[kind] += 16
        self._post((sem, slot.cnts[kind]), reads, writes)

    def barrier(self):
        for e in self.engs:
            for ev in list(self.cur.values()):
                e.wait(ev)


class Ring:
    def __init__(self, items):
        self.items = items
        self.i = 0

    def next(self):
        it = self.items[self.i % len(self.items)]
        self.i += 1
        return it


class RowSplit:
    def __init__(self, prog, name, rows, cols, dt, npieces):
        self.rows, self.pr = rows, rows // npieces
        assert self.pr * npieces == rows and self.pr % 128 == 0
        self.aps = [prog.dram("%s_%d" % (name, i), [self.pr, cols], dt) for i in range(npieces)]

    def __getitem__(self, key):
        rs, cs = key
        r0 = rs.start or 0
        r1 = self.rows if rs.stop is None else rs.stop
        i = r0 // self.pr
        assert i == (r1 - 1) // self.pr, (r0, r1, self.pr)
        return self.aps[i][r0 - i * self.pr:r1 - i * self.pr, cs]

    def all_rows(self, cs):
        return [(i * self.pr, (i + 1) * self.pr, ap[:, cs]) for i, ap in enumerate(self.aps)]


class Buf:
    def __init__(self, t, res):
        self.t = t
        self.res = res


class Prog:
    def __init__(self, cfg):
        self.cfg = cfg
        self.nc = bass.Bass("TRN2", target_bir_lowering=False)
        self.tr = Tracker(self.nc)
        self.slot_pool = []
        self.uid = 0

    def name(self, s):
        self.uid += 1
        return "%s_%d" % (s, self.uid)

    def dram(self, name, shape, dt, kind="Internal"):
        if kind == "Internal" and getattr(self.cfg, "debug", False) and name.split("_")[0] in ("hT", "uT", "qkvT", "attnT", "HT"):
            kind = "ExternalOutput"
        return self.nc.dram_tensor(name, list(shape), dt, kind=kind).ap()

    def get_slot(self, name):
        if self.slot_pool:
            s = self.slot_pool.pop()
            s.name = name
            return s
        return Slot(self.tr, name)

    def sb(self, es, shape, dt, name, slot=False, track=None):
        t = es.enter_context(self.nc.sbuf_tensor(self.name(name), list(shape), dt))
        if slot:
            r = self.get_slot(name)
            es.callback(self._release_slot, r)
        else:
            r = Res(name)
        return Buf(t, r)

    def _release_slot(self, s):
        s.w = None
        s.r = {}
        self.slot_pool.append(s)

    def chunk_pairs(self, sb_t, dr, t0, tw, load, step=8):
        pairs = []
        step = min(step, dr.pr // 128)
        for c0 in range(0, dr.rows // 128, step):
            d = dr[c0 * 128:(c0 + step) * 128, t0:t0 + tw].rearrange("(c p) t -> p c t", p=128)
            sbv = sb_t[:, c0:c0 + step, :]
            pairs.append((sbv, d) if load else (d, sbv))
        return pairs

    def col_pairs(self, dst_ap, src_1d, ncols, step=8):
        return [(dst_ap[:, c0:min(c0 + step, ncols)],
                 src_1d[c0 * 128:min(c0 + step, ncols) * 128].rearrange("(c p) -> p c", p=128))
                for c0 in range(0, ncols, step)]

    def ps(self, es, name, dt=F32, cols=512):
        t = es.enter_context(self.nc.psum_tensor(self.name(name), [128, cols], dt))
        return Buf(t, Res(name))

    def build(self):
        cfg, nc, tr = self.cfg, self.nc, self.tr
        D, B, S, CTX, KC, NTOK, NB, NG = cfg.D, cfg.B, cfg.S, cfg.CTX, cfg.KC, cfg.NTOK, cfg.NB, cfg.NG
        HID, RANK, DEPTH = cfg.HID, cfg.RANK, cfg.DEPTH
        RC = RANK // 128
        nA = (DEPTH + 2) // 3
        nBl = (DEPTH + 1) // 3
        nC = DEPTH // 3
        I = {}

        def ext(name, shape, dt=F32):
            I[name] = self.dram(name, shape, dt, kind="ExternalInput")
            return I[name]

        ext("x", [B * S, D]); ext("c", [B, D]); ext("ctx", [B * CTX, D]); ext("c_ctx", [1, D])
        ext("norm1_g", [DEPTH, D]); ext("norm2_g", [DEPTH, D])
        ext("mod_down", [DEPTH * D, RANK]); ext("mod_up", [DEPTH * RANK, 6 * D]); ext("mod_b", [DEPTH, 6 * D])
        ext("mlp_w1", [DEPTH * D, HID]); ext("mlp_w2", [DEPTH * HID, D])
        ext("a_w_qkv", [nA * D, cfg.NQKV[0]]); ext("a_w_o", [nA * D, D])
        ext("a_q_g", [nA, 128]); ext("a_k_g", [nA, 128])
        ext("b_w_qkv", [max(nBl, 1) * D, 3 * D]); ext("b_w_o", [max(nBl, 1) * D, D])
        ext("rpbx", [max(nBl, 1) * cfg.BH * 18, 64 * 64])
        ext("c_w_qkv", [max(nC, 1) * D, 3 * D]); ext("c_w_o", [max(nC, 1) * D, D])
        ext("c_lam", [max(nC, 1) * 4, 128]); ext("c_subln_g", [max(nC, 1), 256])
        ext("final_g", [1, D])
        ext("k_ident", [128, 128]); ext("k_ones", [128, 128]); ext("k_swap", [128, 128])
        ext("k_cos", [128, NTOK]); ext("k_sin", [128, NTOK])
        ext("k_cm", [128, 64]); ext("k_rm", [128, 4])
        out = self.dram("out", [B * S, D], F32, kind="ExternalOutput")

        big = D * NTOK * 4 > 200 * 2 ** 20
        hT = RowSplit(self, "hT", D, NTOK, F32, 2 if big else 1)
        uT = RowSplit(self, "uT", D, NTOK, BF16, 1)
        qkvT = RowSplit(self, "qkvT", cfg.NQKVMAX, NTOK, BF16, 3 if big else 1)
        attnT = RowSplit(self, "attnT", D, NTOK, BF16, 1)
        HT = RowSplit(self, "HT", HID, NTOK, BF16, HID // D)
        self.I, self.hT, self.uT, self.qkvT, self.attnT, self.HT, self.out = I, hT, uT, qkvT, attnT, HT, out

        def wt(name, K, N):
            return self.dram(name, [N // 256, 128, (K // 128) * 256], BF16)

        W = []
        for i in range(DEPTH):
            kind, slot = i % 3, i // 3
            d = {}
            d["qkv"] = wt("wqkv%d" % i, D, cfg.NQKV[kind])
            d["o"] = wt("wo%d" % i, D, D)
            d["w1"] = wt("w1_%d" % i, D, HID)
            d["w2"] = [wt("w2_%d_%d" % (i, j), D, D) for j in range(HID // D)]
            d["md"] = wt("md%d" % i, D, RANK)
            d["mu"] = wt("mu%d" % i, RANK, 6 * D)
            W.append(d)
        self.W = W

        with ExitStack() as ges:
            self.ident = self.sb(ges, [128, 128], F32, "ident", slot=True)
            self.identb = self.sb(ges, [128, 128], BF16, "identb", slot=True)
            self.onesb = self.sb(ges, [128, 128], BF16, "onesb", slot=True)
            self.swapb = self.sb(ges, [128, 128], BF16, "swapb", slot=True)
            self.sT = self.sb(ges, [128, KC, NG], BF16, "sT")
            self.modT = self.sb(ges, [128, 6 * KC, NG], F32, "modT")
            self.gs1 = self.sb(ges, [128, KC, NG], F32, "gs1")
            self.gs2 = self.sb(ges, [128, KC, NG], F32, "gs2")
            self.n1g = self.sb(ges, [128, KC], F32, "n1g", slot=True)
            self.n2g = self.sb(ges, [128, KC], F32, "n2g", slot=True)
            self.fgT = self.sb(ges, [128, KC], F32, "fgT", slot=True)
            self.mbT = self.sb(ges, [128, 6 * KC], F32, "mbT", slot=True)
            self.epsT = self.sb(ges, [128, 1], F32, "epsT")
            self.dslot = self.get_slot("d2d")

            sp = tr.sp
            tr.dma(sp, self.ident.res, [(self.ident.t[:], I["k_ident"][:, :])], writes=[self.ident.res])
            tr.dma(sp, self.identb.res, [(self.identb.t[:], I["k_ident"][:, :])], writes=[self.identb.res])
            tr.dma(sp, self.onesb.res, [(self.onesb.t[:], I["k_ones"][:, :])], writes=[self.onesb.res])
            tr.dma(sp, self.swapb.res, [(self.swapb.t[:], I["k_swap"][:, :])], writes=[self.swapb.res])
            tr.dma(sp, self.fgT.res, self.col_pairs(self.fgT.t, I["final_g"][0, :], KC),
                   writes=[self.fgT.res], allow_slow_non_contiguous=True)
            tr.op(tr.dve, lambda: nc.vector.memset(self.epsT.t[:], EPS), writes=[self.epsT.res])

            import os
            stop = int(os.environ.get("KSTOP", "100000"))
            steps = [self.phase_cast_weights, self.phase_silu, self.phase_in_transpose, tr.barrier]
            for i in range(DEPTH):
                steps += self.layer_steps(i)
            steps += [self.phase_final]
            for k, f in enumerate(steps):
                if k >= stop:
                    break
                f()
            tr.barrier()
        return nc

    def phase_cast_weights(self):
        cfg, tr, I = self.cfg, self.tr, self.I
        D, HID, RANK, DEPTH = cfg.D, cfg.HID, cfg.RANK, cfg.DEPTH
        qs = [tr.sp, tr.pool]
        self._ci = 0

        def cast(dst, src, K, N):
            kc = K // 128
            step = max(1, min(kc, 8))
            for g in range(N // 256):
                for c0 in range(0, kc, step):
                    o = dst[g, :, c0 * 256:(c0 + step) * 256].rearrange("p (c n) -> p c n", n=256)
                    i = src[c0 * 128:(c0 + step) * 128, g * 256:(g + 1) * 256].rearrange("(c p) n -> p c n", p=128)
                    q = qs[self._ci % 2]
                    self._ci += 1
                    tr.dma(q, self.dslot, [(o, i)])

        for i in range(DEPTH):
            kind, slot = i % 3, i // 3
            wq = [I["a_w_qkv"], I["b_w_qkv"], I["c_w_qkv"]][kind]
            wo = [I["a_w_o"], I["b_w_o"], I["c_w_o"]][kind]
            cast(self.W[i]["qkv"], wq[slot * D:(slot + 1) * D, :], D, cfg.NQKV[kind])
            cast(self.W[i]["o"], wo[slot * D:(slot + 1) * D, :], D, D)
            cast(self.W[i]["w1"], I["mlp_w1"][i * D:(i + 1) * D, :], D, HID)
            for j in range(HID // D):
                cast(self.W[i]["w2"][j], I["mlp_w2"][i * HID + j * D:i * HID + (j + 1) * D, :], D, D)
            cast(self.W[i]["md"], I["mod_down"][i * D:(i + 1) * D, :], D, RANK)
            cast(self.W[i]["mu"], I["mod_up"][i * RANK:(i + 1) * RANK, :], RANK, 6 * D)

    def phase_silu(self):
        cfg, tr, nc, I = self.cfg, self.tr, self.nc, self.I
        KC, NG, B = cfg.KC, cfg.NG, cfg.B
        with ExitStack() as es:
            cT = self.sb(es, [128, NG, KC], F32, "cT", slot=True)
            pairs = []
            for g in range(NG):
                src = I["c"][g, :] if g < B else I["c_ctx"][0, :]
                pairs += self.col_pairs(cT.t[:, g, :], src, KC)
            tr.dma(tr.sp, cT.res, pairs, writes=[cT.res], allow_slow_non_contiguous=True)
            for g in range(NG):
                tr.op(tr.act, lambda g=g: nc.scalar.activation(out=self.sT.t[:, :, g], in_=cT.t[:, g, :], func=AF.Silu),
                      reads=[cT.res], writes=[self.sT.res])
            tr.barrier()

    def tok_src(self, b, j0, n):
        cfg, I = self.cfg, self.I
        if j0 < cfg.CTX:
            return I["ctx"][b * cfg.CTX + j0:b * cfg.CTX + j0 + n, :]
        s0 = j0 - cfg.CTX
        return I["x"][b * cfg.S + s0:b * cfg.S + s0 + n, :]

    def phase_in_transpose(self):
        cfg, tr, nc = self.cfg, self.tr, self.nc
        D, KC, NB, B = cfg.D, cfg.KC, cfg.NB, cfg.B
        with ExitStack() as es:
            xin = [self.sb(es, [128, 2, D], F32, "xin", slot=True) for _ in range(2)]
            stg = [self.sb(es, [128, 256], F32, "tstg", slot=True) for _ in range(4)]
            pss = [self.ps(es, "tps") for _ in range(4)]
            xr, sr, pr = Ring(xin), Ring(stg), Ring(pss)
            k = 0
            for b in range(B):
                for j0 in range(0, NB, 256):
                    xb = xr.next()
                    src = self.tok_src(b, j0, 256).rearrange("(j p) d -> p j d", p=128)
                    tr.dma(tr.sp, xb.res, [(xb.t[:], src)], writes=[xb.res])
                    t0 = b * NB + j0
                    for c in range(KC):
                        pb = pr.next()
                        tr.op(tr.pe, lambda: [nc.tensor.transpose(out=pb.t[:, j * 128:(j + 1) * 128],
                                                                  in_=xb.t[:, j, c * 128:(c + 1) * 128],
                                                                  identity=self.ident.t[:]) for j in range(2)],
                              reads=[xb.res, self.ident.res], writes=[pb.res])
                        sb_ = sr.next()
                        if k % 2 == 0:
                            tr.op(tr.act, lambda: nc.scalar.copy(out=sb_.t[:], in_=pb.t[:, 0:256]),
                                  reads=[pb.res], writes=[sb_.res])
                        else:
                            tr.op(tr.dve, lambda: nc.vector.tensor_copy(out=sb_.t[:], in_=pb.t[:, 0:256]),
                                  reads=[pb.res], writes=[sb_.res])
                        k += 1
                        tr.dma(tr.pool, sb_.res, [(self.hT[c * 128:(c + 1) * 128, t0:t0 + 256], sb_.t[:])],
                               reads=[sb_.res])
            tr.barrier()

    def groups_in(self, t0, n):
        cfg = self.cfg
        res = []
        t = t0
        while t < t0 + n:
            b, j = divmod(t, cfg.NB)
            if j < cfg.CTX:
                e = min(t0 + n, b * cfg.NB + cfg.CTX)
                g = cfg.B
            else:
                e = min(t0 + n, (b + 1) * cfg.NB)
                g = b
            res.append((t - t0, e - t, g))
            t = e
        return res

    def phase_mod(self, i):
        cfg, tr, nc, I = self.cfg, self.tr, self.nc, self.I
        D, KC, NG, RANK = cfg.D, cfg.KC, cfg.NG, cfg.RANK
        RC = RANK // 128
        Wd, Wu = self.W[i]["md"], self.W[i]["mu"]
        with ExitStack() as es:
            wds = [self.sb(es, [128, KC * 256], BF16, "wd", slot=True) for _ in range(2)]
            MU = 4
            wus = [self.sb(es, [128, MU, RC * 256], BF16, "wu", slot=True) for _ in range(2)]
            rT = self.sb(es, [128, RC, NG], BF16, "rT")
            pss = [self.ps(es, "mps") for _ in range(2)]
            wdr, wur, pr = Ring(wds), Ring(wus), Ring(pss)
            tr.dma(tr.sp, self.n1g.res, self.col_pairs(self.n1g.t, I["norm1_g"][i, :], KC),
                   writes=[self.n1g.res], allow_slow_non_contiguous=True)
            tr.dma(tr.sp, self.n2g.res, self.col_pairs(self.n2g.t, I["norm2_g"][i, :], KC),
                   writes=[self.n2g.res], allow_slow_non_contiguous=True)
            tr.dma(tr.sp, self.mbT.res, self.col_pairs(self.mbT.t, I["mod_b"][i, :], 6 * KC),
                   writes=[self.mbT.res], allow_slow_non_contiguous=True)
            for rp in range(RC // 2):
                wb = wdr.next()
                tr.dma(tr.sp, wb.res, [(wb.t[:], Wd[rp, :, :])], writes=[wb.res])
                w3 = wb.t[:].rearrange("p (c n) -> p c n", n=256)
                for h2 in range(2):
                    pb = pr.next()
                    tr.op(tr.pe, lambda: [nc.tensor.matmul(pb.t[:, 0:NG], w3[:, k, h2 * 128:(h2 + 1) * 128],
                                                           self.sT.t[:, k, :], start=(k == 0), stop=(k == KC - 1))
                                          for k in range(KC)],
                          reads=[wb.res, self.sT.res], writes=[pb.res])
                    tr.op(tr.dve, lambda: nc.vector.tensor_copy(out=rT.t[:, rp * 2 + h2, :], in_=pb.t[:, 0:NG]),
                          reads=[pb.res], writes=[rT.res])
            npair = 6 * D // 256
            for m0 in range(0, npair, MU):
                wb = wur.next()
                mu = min(MU, npair - m0)
                tr.dma(tr.sp, wb.res, [(wb.t[:, 0:mu, :], Wu[m0:m0 + mu, :, :].rearrange("m p x -> p m x"))],
                       writes=[wb.res])
                for mi in range(mu):
                    w3 = wb.t[:, mi, :].rearrange("p (c n) -> p c n", n=256)
                    for h2 in range(2):
                        mc = (m0 + mi) * 2 + h2
                        pb = pr.next()
                        tr.op(tr.pe, lambda: [nc.tensor.matmul(pb.t[:, 0:NG], w3[:, k, h2 * 128:(h2 + 1) * 128],
                                                               rT.t[:, k, :], start=(k == 0), stop=(k == RC - 1))
                                              for k in range(RC)],
                              reads=[wb.res, rT.res], writes=[pb.res])
                        tr.op(tr.dve, lambda: nc.vector.tensor_scalar(out=self.modT.t[:, mc, :], in0=pb.t[:, 0:NG],
                                                                      scalar1=self.mbT.t[:, mc:mc + 1], scalar2=None,
                                                                      op0=ALU.add),
                              reads=[pb.res, self.mbT.res], writes=[self.modT.res])
            for (gs, ng, mi) in ((self.gs1, self.n1g, 1), (self.gs2, self.n2g, 4)):
                for g in range(NG):
                    tr.op(tr.dve, lambda: nc.vector.scalar_tensor_tensor(
                        out=gs.t[:, :, g], in0=self.modT.t[:, mi * KC:(mi + 1) * KC, g], scalar=1.0, in1=ng.t[:],
                        op0=ALU.add, op1=ALU.mult), reads=[self.modT.res, ng.res], writes=[gs.res])
            tr.barrier()

    def mod_col(self, mi, c, g):
        return self.modT.t[:, mi * self.cfg.KC + c, g:g + 1]

    def rstd_from_psum(self, out_ap, ps_ap, n, reads, writes):
        tr, nc = self.tr, self.nc
        tr.op(tr.act, lambda: nc.scalar.activation(out=out_ap, in_=ps_ap, func=AF.Sqrt, bias=self.epsT.t[:, 0:1],
                                                   scale=1.0 / n), reads=list(reads) + [self.epsT.res], writes=writes)
        tr.op(tr.dve, lambda: nc.vector.reciprocal(out=out_ap, in_=out_ap), reads=writes, writes=writes)

    def phase_norm(self, gs, shift_mi, dst):
        cfg, tr, nc = self.cfg, self.tr, self.nc
        D, KC, NTOK = cfg.D, cfg.KC, cfg.NTOK
        TW = 256
        with ExitStack() as es:
            hin = [self.sb(es, [128, KC, TW], F32, "hin", slot=True) for _ in range(2)]
            sq = [self.sb(es, [128, KC, TW], BF16, "sq") for _ in range(2)]
            tmp = [self.sb(es, [128, TW], F32, "ntmp") for _ in range(4)]
            uo = [self.sb(es, [128, KC, TW], BF16, "uo", slot=True) for _ in range(2)]
            rs = [self.sb(es, [128, TW], F32, "rstd") for _ in range(2)]
            pss = [self.ps(es, "nps") for _ in range(2)]
            hr, sqr, tmr, uor, rsr, pr = Ring(hin), Ring(sq), Ring(tmp), Ring(uo), Ring(rs), Ring(pss)
            for t0 in range(0, NTOK, TW):
                (_, _, g), = self.groups_in(t0, TW)
                hb, sb_, ub, rb, pb = hr.next(), sqr.next(), uor.next(), rsr.next(), pr.next()
                tr.dma(tr.sp, hb.res, self.chunk_pairs(hb.t, self.hT, t0, TW, True), writes=[hb.res])
                tr.op(tr.act, lambda: nc.scalar.activation(out=sb_.t[:], in_=hb.t[:], func=AF.Square),
                      reads=[hb.res], writes=[sb_.res])
                tr.op(tr.pe, lambda: [nc.tensor.matmul(pb.t[:, 0:TW], self.onesb.t[:], sb_.t[:, k, :],
                                                       start=(k == 0), stop=(k == KC - 1)) for k in range(KC)],
                      reads=[sb_.res, self.onesb.res], writes=[pb.res])
                self.rstd_from_psum(rb.t[:], pb.t[:, 0:TW], D, [pb.res], [rb.res])
                for c in range(KC):
                    tb = tmr.next()
                    tr.op(tr.dve, lambda: nc.vector.scalar_tensor_tensor(
                        out=tb.t[:], in0=hb.t[:, c, :], scalar=gs.t[:, c, g:g + 1], in1=rb.t[:],
                        op0=ALU.mult, op1=ALU.mult), reads=[hb.res, rb.res, gs.res], writes=[tb.res])
                    tr.op(tr.act, lambda: nc.scalar.activation(out=ub.t[:, c, :], in_=tb.t[:], func=AF.Identity,
                                                               bias=self.mod_col(shift_mi, c, g), scale=1.0),
                          reads=[tb.res, self.modT.res], writes=[ub.res])
                tr.dma(tr.pool, ub.res, self.chunk_pairs(ub.t, dst, t0, TW, False), reads=[ub.res])
            tr.barrier()

    def gemm(self, X, Wt, K, N, epi, epi_alloc=None):
        cfg, tr, nc = self.cfg, self.tr, self.nc
        NTOK = cfg.NTOK
        kc = K // 128
        XG = min(8, kc)
        ngrp = kc // XG
        TT = 1024
        with ExitStack() as es:
            nring = ngrp + max(1, ngrp // 4)
            xs = [self.sb(es, [128, XG, TT], BF16, "xs", slot=True) for _ in range(nring)]
            ws = [self.sb(es, [128, kc * 256], BF16, "ws", slot=True) for _ in range(3)]
            psets = [[self.ps(es, "gps") for _ in range(2)] for _ in range(3)]
            xr, wr, pr = Ring(xs), Ring(ws), Ring(psets)
            st = epi_alloc(es) if epi_alloc else None
            for t0 in range(0, NTOK, TT):
                tw = min(TT, NTOK - t0)
                nh = tw // 512
                xg = []
                for gi in range(ngrp):
                    xb = xr.next()
                    tr.dma(tr.sp, xb.res, [(xb.t[:, :, 0:tw],
                                            X[gi * XG * 128:(gi + 1) * XG * 128, t0:t0 + tw].rearrange(
                                                "(c p) t -> p c t", p=128))], writes=[xb.res])
                    xg.append(xb)
                for npair in range(N // 256):
                    wb = wr.next()
                    tr.dma(tr.sp, wb.res, [(wb.t[:], Wt[npair, :, :])], writes=[wb.res])
                    w3 = wb.t[:].rearrange("p (c n) -> p c n", n=256)
                    for h2 in range(2):
                        nch = npair * 2 + h2
                        banks = pr.next()
                        for gi in range(ngrp):
                            xb = xg[gi]
                            tr.op(tr.pe, lambda: [nc.tensor.matmul(banks[hf].t[:, 0:512],
                                                                   w3[:, gi * XG + k, h2 * 128:(h2 + 1) * 128],
                                                                   xb.t[:, k, hf * 512:(hf + 1) * 512],
                                                                   start=(gi == 0 and k == 0),
                                                                   stop=(gi == ngrp - 1 and k == XG - 1))
                                                  for k in range(XG) for hf in range(nh)],
                                  reads=[wb.res, xb.res], writes=[banks[hf].res for hf in range(nh)])
                        epi(st, nch, t0, banks[:nh])
            tr.barrier()

    def epi_store_alloc(self, es):
        st = {"stg": Ring([self.sb(es, [128, 1024], BF16, "stg", slot=True) for _ in range(3)]), "k": 0}
        return st

    def evac(self, st, out_ap, in_ap, reads, writes):
        tr, nc = self.tr, self.nc
        st["k"] += 1
        import os
        if os.environ.get("KEVAC") == "dve":
            st["k"] = 1
        if os.environ.get("KEVAC") == "act":
            st["k"] = 0
        if st["k"] % 2 == 0:
            tr.op(tr.act, lambda: nc.scalar.copy(out=out_ap, in_=in_ap), reads=reads, writes=writes)
        else:
            tr.op(tr.dve, lambda: nc.vector.tensor_copy(out=out_ap, in_=in_ap), reads=reads, writes=writes)

    def make_epi_plain(self, dst, row_of=lambda nch: nch * 128):
        tr = self.tr

        def epi(st, nch, t0, banks):
            sg = st["stg"].next()
            for hf, bk in enumerate(banks):
                self.evac(st, sg.t[:, hf * 512:(hf + 1) * 512], bk.t[:, 0:512], [bk.res], [sg.res])
            tw = 512 * len(banks)
            r0 = row_of(nch)
            tr.dma(tr.pool, sg.res, [(dst[r0:r0 + 128, t0:t0 + tw], sg.t[:, 0:tw])], reads=[sg.res])
        return epi

    def make_epi_relu2(self, dst):
        tr, nc = self.tr, self.nc

        def alloc(es):
            st = self.epi_store_alloc(es)
            st["sq"] = Ring([self.sb(es, [128, 512], F32, "r2") for _ in range(3)])
            return st

        def epi(st, nch, t0, banks):
            sg = st["stg"].next()
            for hf, bk in enumerate(banks):
                s2 = st["sq"].next()
                tr.op(tr.act, lambda: nc.scalar.activation(out=s2.t[:], in_=bk.t[:, 0:512], func=AF.Square),
                      reads=[bk.res], writes=[s2.res])
                tr.op(tr.dve, lambda: nc.vector.scalar_tensor_tensor(
                    out=sg.t[:, hf * 512:(hf + 1) * 512], in0=bk.t[:, 0:512], scalar=0.0, in1=s2.t[:],
                    op0=ALU.is_gt, op1=ALU.mult), reads=[bk.res, s2.res], writes=[sg.res])
            tw = 512 * len(banks)
            tr.dma(tr.pool, sg.res, [(dst[nch * 128:(nch + 1) * 128, t0:t0 + tw], sg.t[:, 0:tw])], reads=[sg.res])
        return alloc, epi

    def make_epi_resid(self, gate_mi):
        tr, nc = self.tr, self.nc

        def alloc(es):
            return {"h": Ring([self.sb(es, [128, 1024], F32, "hres", slot=True) for _ in range(3)])}

        def epi(st, nch, t0, banks):
            hb = st["h"].next()
            tw = 512 * len(banks)
            dr = self.hT[nch * 128:(nch + 1) * 128, t0:t0 + tw]
            tr.dma(tr.sp, hb.res, [(hb.t[:, 0:tw], dr)], writes=[hb.res])
            for hf, bk in enumerate(banks):
                for (o, n, g) in self.groups_in(t0 + hf * 512, 512):
                    a, b_ = hf * 512 + o, hf * 512 + o + n
                    tr.op(tr.dve, lambda: nc.vector.scalar_tensor_tensor(
                        out=hb.t[:, a:b_], in0=bk.t[:, o:o + n], scalar=self.mod_col(gate_mi, nch, g),
                        in1=hb.t[:, a:b_], op0=ALU.mult, op1=ALU.add),
                        reads=[bk.res, self.modT.res, hb.res], writes=[hb.res])
            tr.dma(tr.pool, hb.res, [(dr, hb.t[:, 0:tw])], reads=[hb.res])
        return alloc, epi

    def make_epi_qkv(self, kind, slot):
        cfg, tr, nc, I = self.cfg, self.tr, self.nc, self.I
        D = cfg.D
        plain = self.make_epi_plain(self.qkvT)
        if kind == 1:
            return self.epi_store_alloc, plain
        nq = cfg.AH if kind == 0 else D // 128
        nk = cfg.AKV if kind == 0 else D // 128

        def alloc(es):
            st = self.epi_store_alloc(es)
            st["cos"] = Ring([self.sb(es, [128, 1024], F32, "cos", slot=True) for _ in range(2)])
            st["sin"] = Ring([self.sb(es, [128, 1024], F32, "sin", slot=True) for _ in range(2)])
            st["t0"] = None
            st["qg"] = Ring([self.sb(es, [128, 512], BF16, "qg") for _ in range(2)])
            st["t1"] = Ring([self.sb(es, [128, 512], F32, "t1") for _ in range(2)])
            st["t2"] = Ring([self.sb(es, [128, 512], F32, "t2") for _ in range(2)])
            st["ps"] = Ring([self.ps(es, "eps") for _ in range(2)])
            if kind == 0:
                st["sq"] = Ring([self.sb(es, [128, 512], BF16, "esq") for _ in range(2)])
                st["rs"] = Ring([self.sb(es, [128, 512], F32, "ers") for _ in range(2)])
                st["gain"] = self.sb(es, [128, 2], F32, "gain", slot=True)
                tr.dma(tr.sp, st["gain"].res,
                       [(st["gain"].t[:, 0:1], I["a_q_g"][slot, :].rearrange("(p o) -> p o", o=1)),
                        (st["gain"].t[:, 1:2], I["a_k_g"][slot, :].rearrange("(p o) -> p o", o=1))],
                       writes=[st["gain"].res], allow_slow_non_contiguous=True)
            return st

        def epi(st, nch, t0, banks):
            if nch >= nq + nk:
                return plain(st, nch, t0, banks)
            tw = 512 * len(banks)
            if st["t0"] != t0:
                st["t0"] = t0
                st["cb"], st["sb"] = st["cos"].next(), st["sin"].next()
                tr.dma(tr.sp, st["cb"].res, [(st["cb"].t[:, 0:tw], I["k_cos"][:, t0:t0 + tw])], writes=[st["cb"].res])
                tr.dma(tr.sp, st["sb"].res, [(st["sb"].t[:, 0:tw], I["k_sin"][:, t0:t0 + tw])], writes=[st["sb"].res])
            cb, sb_ = st["cb"], st["sb"]
            sg = st["stg"].next()
            for hf, bk in enumerate(banks):
                qg, t1, t2, pr = st["qg"].next(), st["t1"].next(), st["t2"].next(), st["ps"].next()
                cs = slice(hf * 512, (hf + 1) * 512)
                if kind == 0:
                    sq, rs = st["sq"].next(), st["rs"].next()
                    gcol = st["gain"].t[:, 0:1] if nch < nq else st["gain"].t[:, 1:2]
                    tr.op(tr.act, lambda: nc.scalar.activation(out=sq.t[:], in_=bk.t[:, 0:512], func=AF.Square),
                          reads=[bk.res], writes=[sq.res])
                    tr.op(tr.act, lambda: nc.scalar.activation(out=qg.t[:], in_=bk.t[:, 0:512], func=AF.Copy,
                                                               scale=gcol),
                          reads=[bk.res, st["gain"].res], writes=[qg.res])
                    tr.op(tr.pe, lambda: nc.tensor.matmul(pr.t[:, 0:512], self.onesb.t[:], sq.t[:], start=True, stop=True),
                          reads=[sq.res, self.onesb.res], writes=[pr.res])
                    self.rstd_from_psum(rs.t[:], pr.t[:, 0:512], 128, [pr.res], [rs.res])
                else:
                    self.evac(st, qg.t[:], bk.t[:, 0:512], [bk.res], [qg.res])
                tr.op(tr.pe, lambda: nc.tensor.matmul(pr.t[:, 0:512], self.swapb.t[:], qg.t[:], start=True, stop=True),
                      reads=[qg.res, self.swapb.res, pr.res], writes=[pr.res])
                tr.op(tr.dve, lambda: nc.vector.tensor_tensor(out=t1.t[:], in0=qg.t[:], in1=cb.t[:, cs], op=ALU.mult),
                      reads=[qg.res, cb.res], writes=[t1.res])
                tr.op(tr.dve, lambda: nc.vector.tensor_tensor(out=t2.t[:], in0=pr.t[:, 0:512], in1=sb_.t[:, cs], op=ALU.mult),
                      reads=[pr.res, sb_.res], writes=[t2.res])
                if kind == 0:
                    tr.op(tr.pool, lambda: nc.gpsimd.tensor_tensor(out=t1.t[:], in0=t1.t[:], in1=t2.t[:], op=ALU.add),
                          reads=[t1.res, t2.res], writes=[t1.res])
                    tr.op(tr.dve, lambda: nc.vector.tensor_tensor(out=sg.t[:, cs], in0=t1.t[:], in1=rs.t[:], op=ALU.mult),
                          reads=[t1.res, rs.res], writes=[sg.res])
                else:
                    tr.op(tr.pool, lambda: nc.gpsimd.tensor_tensor(out=sg.t[:, cs], in0=t1.t[:], in1=t2.t[:], op=ALU.add),
                          reads=[t1.res, t2.res], writes=[sg.res])
            tr.dma(tr.pool, sg.res, [(self.qkvT[nch * 128:(nch + 1) * 128, t0:t0 + tw], sg.t[:, 0:tw])], reads=[sg.res])
        return alloc, epi

    def load_vT_transposed(self, st, vdst, row0, tok0, nblk, ncol=1, col0=0):
        tr, nc = self.tr, self.nc
        vt = st["vt"].next()
        tr.dma(tr.sp, vt.res, [(vt.t[:, 0:nblk * 128], self.qkvT[row0:row0 + 128, tok0:tok0 + nblk * 128])],
               writes=[vt.res])
        for b0 in range(0, nblk, 8):
            nb_ = min(8, nblk - b0)
            pb = st["tps"].next()
            tr.op(tr.pe, lambda: [nc.tensor.transpose(out=pb.t[:, j * 128:(j + 1) * 128],
                                                      in_=vt.t[:, (b0 + j) * 128:(b0 + j + 1) * 128],
                                                      identity=self.identb.t[:]) for j in range(nb_)],
                  reads=[vt.res, self.identb.res], writes=[pb.res])
            tr.op(tr.dve, lambda: nc.vector.tensor_copy(
                out=vdst.t[:, b0:b0 + nb_, col0 * 128:(col0 + 1) * 128],
                in_=pb.t[:, 0:nb_ * 128].rearrange("p (j d) -> p j d", d=128)),
                reads=[pb.res], writes=[vdst.res])

    def attn_common_alloc(self, es, nbk, vcols=128):
        st = {}
        st["kT"] = Ring([self.sb(es, [128, nbk * 128], BF16, "kT", slot=True) for _ in range(2)])
        st["vt"] = Ring([self.sb(es, [128, nbk * 128], BF16, "vt", slot=True) for _ in range(2)])
        st["v"] = Ring([self.sb(es, [128, nbk, vcols], BF16, "v") for _ in range(2)])
        st["tps"] = Ring([self.ps(es, "tpsb", dt=BF16, cols=1024)])
        st["ostg"] = Ring([self.sb(es, [128, 512], BF16, "ostg", slot=True) for _ in range(3)])
        st["rden"] = Ring([self.sb(es, [128, 512], F32, "rden") for _ in range(2)])
        return st

    def attend_block(self, st, qT_ap, qres, nq, kT, v, ktiles, sc_ring, p_ring, o_bank, d_bank, scale, exp_fn=None):
        tr, nc = self.tr, self.nc
        n = len(ktiles)
        pend = []

        def issue_pv(idx, pb):
            kt = ktiles[idx]
            tr.op(tr.pe, lambda: [nc.tensor.matmul(o_bank.t[:, 0:nq], v.t[:, kt, 0:128], pb.t[:, 0:nq],
                                                   start=(idx == 0), stop=(idx == n - 1)),
                                  nc.tensor.matmul(d_bank.t[:, 0:nq], self.onesb.t[:], pb.t[:, 0:nq],
                                                   start=(idx == 0), stop=(idx == n - 1))],
                  reads=[v.res, pb.res, self.onesb.res], writes=[o_bank.res, d_bank.res])

        for idx, kt in enumerate(ktiles):
            sb_ = sc_ring.next()
            tr.op(tr.pe, lambda: nc.tensor.matmul(sb_.t[:, 0:nq], kT.t[:, kt * 128:(kt + 1) * 128], qT_ap,
                                                  start=True, stop=True),
                  reads=[kT.res, qres], writes=[sb_.res])
            pb = p_ring.next()
            if exp_fn is not None and exp_fn(idx, kt, sb_, pb):
                pass
            else:
                tr.op(tr.act, lambda: nc.scalar.activation(out=pb.t[:, 0:nq], in_=sb_.t[:, 0:nq], func=AF.Exp, scale=scale),
                      reads=[sb_.res], writes=[pb.res])
            pend.append((idx, pb))
            if len(pend) > 1:
                issue_pv(*pend.pop(0))
        while pend:
            issue_pv(*pend.pop(0))

    def finish_o(self, st, o_bank, d_bank, nq, dst_rows, tok0):
        tr, nc = self.tr, self.nc
        rd, og = st["rden"].next(), st["ostg"].next()
        tr.op(tr.dve, lambda: nc.vector.reciprocal(out=rd.t[:, 0:nq], in_=d_bank.t[:, 0:nq]),
              reads=[d_bank.res], writes=[rd.res])
        tr.op(tr.dve, lambda: nc.vector.tensor_tensor(out=og.t[:, 0:nq], in0=o_bank.t[:, 0:nq], in1=rd.t[:, 0:nq],
                                                      op=ALU.mult), reads=[o_bank.res, rd.res], writes=[og.res])
        tr.dma(tr.pool, og.res, [(self.attnT[dst_rows:dst_rows + 128, tok0:tok0 + nq], og.t[:, 0:nq])], reads=[og.res])

    def phase_attn_gqa(self, with_ctx):
        cfg, tr, nc = self.cfg, self.tr, self.nc
        B, S, CTX, NB, D = cfg.B, cfg.S, cfg.CTX, cfg.NB, cfg.D
        nbk = NB // 128
        grp = cfg.AH // cfg.AKV
        scale = 128 ** -0.5
        kr0 = cfg.AH * 128
        vr0 = kr0 + cfg.AKV * 128
        with ExitStack() as es:
            st = self.attn_common_alloc(es, nbk)
            qTs = Ring([self.sb(es, [128, NB], BF16, "qT", slot=True) for _ in range(2)])
            sc = Ring([self.ps(es, "sc") for _ in range(3)])
            pr = Ring([self.sb(es, [128, 512], BF16, "pT") for _ in range(4)])
            ob = Ring([self.ps(es, "ob") for _ in range(2)])
            db = Ring([self.ps(es, "db") for _ in range(2)])
            for b in range(B):
                tb = b * NB
                for kv in range(cfg.AKV):
                    kT, v = st["kT"].next(), st["v"].next()
                    tr.dma(tr.sp, kT.res, [(kT.t[:], self.qkvT[kr0 + kv * 128:kr0 + (kv + 1) * 128, tb:tb + NB])],
                           writes=[kT.res])
                    self.load_vT_transposed(st, v, vr0 + kv * 128, tb, nbk)
                    for g in range(grp):
                        h = kv * grp + g
                        qT = qTs.next()
                        tr.dma(tr.sp, qT.res, [(qT.t[:], self.qkvT[h * 128:(h + 1) * 128, tb:tb + NB])], writes=[qT.res])
                        tiles = [(CTX + q0, 512, list(range(nbk))) for q0 in range(0, S, 512)]
                        if with_ctx:
                            tiles.append((0, CTX, list(range(CTX // 128))))
                        for (q0, nq, kts) in tiles:
                            o_b, d_b = ob.next(), db.next()
                            self.attend_block(st, qT.t[:, q0:q0 + nq], qT.res, nq, kT, v, kts, sc, pr, o_b, d_b, scale)
                            self.finish_o(st, o_b, d_b, nq, h * 128, tb + q0)
            tr.barrier()

    def phase_attn_diff(self, slot, lambda_init, with_ctx):
        cfg, tr, nc, I = self.cfg, self.tr, self.nc, self.I
        B, S, CTX, NB, D = cfg.B, cfg.S, cfg.CTX, cfg.NB, cfg.D
        nbk = NB // 128
        scale = 128 ** -0.5
        with ExitStack() as es:
            st = self.attn_common_alloc(es, nbk, vcols=256)
            kT2 = Ring([self.sb(es, [128, nbk * 128], BF16, "kT2", slot=True) for _ in range(2)])
            qTs = Ring([self.sb(es, [128, 2, NB], BF16, "qTd", slot=True) for _ in range(2)])
            sc = Ring([self.ps(es, "sc") for _ in range(1)])
            pr = Ring([self.sb(es, [128, 512], BF16, "pT") for _ in range(4)])
            obs = [[self.ps(es, "ob") for _ in range(2)] for _ in range(2)]
            dbs = [self.ps(es, "db") for _ in range(2)]
            lam4 = self.sb(es, [128, 4], F32, "lam4", slot=True)
            lamw = self.sb(es, [128, 4], F32, "lamw")
            nlam = self.sb(es, [128, 1], F32, "nlam")
            sub = self.sb(es, [128, 2], F32, "subg", slot=True)
            tA = Ring([self.sb(es, [128, 512], F32, "tA") for _ in range(4)])
            oc = Ring([self.sb(es, [128, 2, 512], F32, "oc") for _ in range(2)])
            osq = Ring([self.sb(es, [128, 2, 512], BF16, "osq") for _ in range(2)])
            rs = Ring([self.sb(es, [128, 512], F32, "drs") for _ in range(2)])
            tr.dma(tr.sp, lam4.res, [(lam4.t[:, j:j + 1], I["c_lam"][slot * 4 + j, :].rearrange("(p o) -> p o", o=1))
                                     for j in range(4)], writes=[lam4.res], allow_slow_non_contiguous=True)
            tr.dma(tr.sp, sub.res, [(sub.t[:, j:j + 1], I["c_subln_g"][slot, j * 128:(j + 1) * 128].rearrange("(p o) -> p o", o=1))
                                    for j in range(2)], writes=[sub.res], allow_slow_non_contiguous=True)
            tr.op(tr.dve, lambda: nc.vector.tensor_tensor(out=lamw.t[:, 0:1], in0=lam4.t[:, 0:1], in1=lam4.t[:, 1:2], op=ALU.mult),
                  reads=[lam4.res], writes=[lamw.res])
            tr.op(tr.dve, lambda: nc.vector.tensor_tensor(out=lamw.t[:, 1:2], in0=lam4.t[:, 2:3], in1=lam4.t[:, 3:4], op=ALU.mult),
                  reads=[lam4.res, lamw.res], writes=[lamw.res])
            lamb = self.sb(es, [128, 2], BF16, "lamb")
            onesf = self.sb(es, [128, 128], F32, "onesf")
            tr.op(tr.dve, lambda: nc.vector.memset(onesf.t[:], 1.0), writes=[onesf.res])
            d0 = dbs[0]
            tr.op(tr.pe, lambda: nc.tensor.matmul(d0.t[:, 0:2], onesf.t[:], lamw.t[:, 0:2], start=True, stop=True),
                  reads=[onesf.res, lamw.res], writes=[d0.res])
            tr.op(tr.act, lambda: nc.scalar.activation(out=lamw.t[:, 2:4], in_=d0.t[:, 0:2], func=AF.Exp),
                  reads=[d0.res, lamw.res], writes=[lamw.res])
            tr.op(tr.dve, lambda: nc.vector.tensor_tensor(out=nlam.t[:], in0=lamw.t[:, 3:4], in1=lamw.t[:, 2:3], op=ALU.subtract),
                  reads=[lamw.res], writes=[nlam.res])
            tr.op(tr.dve, lambda: nc.vector.tensor_scalar(out=nlam.t[:], in0=nlam.t[:], scalar1=-float(lambda_init),
                                                          scalar2=None, op0=ALU.add), reads=[nlam.res], writes=[nlam.res])
            tr.op(tr.dve, lambda: nc.vector.tensor_scalar(out=sub.t[:], in0=sub.t[:], scalar1=float(1.0 - lambda_init),
                                                          scalar2=None, op0=ALU.mult), reads=[sub.res], writes=[sub.res])
            for b in range(B):
                tb = b * NB
                for h in range(cfg.CH):
                    kTa, kTb, v, qT = st["kT"].next(), kT2.next(), st["v"].next(), qTs.next()
                    r = h * 256
                    tr.dma(tr.sp, kTa.res, [(kTa.t[:], self.qkvT[D + r:D + r + 128, tb:tb + NB])], writes=[kTa.res])
                    tr.dma(tr.sp, kTb.res, [(kTb.t[:], self.qkvT[D + r + 128:D + r + 256, tb:tb + NB])], writes=[kTb.res])
                    tr.dma(tr.sp, qT.res, [(qT.t[:, 0, :], self.qkvT[r:r + 128, tb:tb + NB]),
                                           (qT.t[:, 1, :], self.qkvT[r + 128:r + 256, tb:tb + NB])], writes=[qT.res])
                    for j in range(2):
                        self.load_vT_transposed(st, v, 2 * D + r + j * 128, tb, nbk, col0=j)
                    kTs = [kTa, kTb]
                    tiles = [(CTX + q0, 512, list(range(nbk))) for q0 in range(0, S, 512)]
                    if with_ctx:
                        tiles.append((0, CTX, list(range(CTX // 128))))
                    for (q0, nq, kts) in tiles:
                        n = len(kts)
                        pend = []

                        def issue_pv(idx, kt, cpt, pb):
                            tr.op(tr.pe, lambda: [nc.tensor.matmul(obs[cpt][j].t[:, 0:nq], v.t[:, kt, j * 128:(j + 1) * 128],
                                                                   pb.t[:, 0:nq], start=(idx == 0), stop=(idx == n - 1))
                                                  for j in range(2)] +
                                  [nc.tensor.matmul(dbs[cpt].t[:, 0:nq], self.onesb.t[:], pb.t[:, 0:nq],
                                                    start=(idx == 0), stop=(idx == n - 1))],
                                  reads=[v.res, pb.res, self.onesb.res],
                                  writes=[obs[cpt][0].res, obs[cpt][1].res, dbs[cpt].res])

                        for idx, kt in enumerate(kts):
                            for cpt in range(2):
                                sb_ = sc.next()
                                tr.op(tr.pe, lambda: nc.tensor.matmul(sb_.t[:, 0:nq], kTs[cpt].t[:, kt * 128:(kt + 1) * 128],
                                                                      qT.t[:, cpt, q0:q0 + nq], start=True, stop=True),
                                      reads=[kTs[cpt].res, qT.res], writes=[sb_.res])
                                pb = pr.next()
                                tr.op(tr.act, lambda: nc.scalar.activation(out=pb.t[:, 0:nq], in_=sb_.t[:, 0:nq], func=AF.Exp,
                                                                           scale=scale), reads=[sb_.res], writes=[pb.res])
                                pend.append((idx, kt, cpt, pb))
                                if len(pend) > 1:
                                    issue_pv(*pend.pop(0))
                        while pend:
                            issue_pv(*pend.pop(0))
                        rd0, rd1 = tA.next(), tA.next()
                        tr.op(tr.dve, lambda: nc.vector.reciprocal(out=rd0.t[:, 0:nq], in_=dbs[0].t[:, 0:nq]),
                              reads=[dbs[0].res], writes=[rd0.res])
                        tr.op(tr.dve, lambda: nc.vector.reciprocal(out=rd1.t[:, 0:nq], in_=dbs[1].t[:, 0:nq]),
                              reads=[dbs[1].res], writes=[rd1.res])
                        tr.op(tr.dve, lambda: nc.vector.tensor_scalar(out=rd1.t[:, 0:nq], in0=rd1.t[:, 0:nq], scalar1=nlam.t[:, 0:1],
                                                                      scalar2=None, op0=ALU.mult),
                              reads=[rd1.res, nlam.res], writes=[rd1.res])
                        ocb, sqb, rsb = oc.next(), osq.next(), rs.next()
                        for j in range(2):
                            ta = tA.next()
                            tr.op(tr.dve, lambda: nc.vector.tensor_tensor(out=ta.t[:, 0:nq], in0=obs[1][j].t[:, 0:nq], in1=rd1.t[:, 0:nq],
                                                                          op=ALU.mult), reads=[obs[1][j].res, rd1.res], writes=[ta.res])
                            tr.op(tr.dve, lambda: nc.vector.tensor_tensor(out=ocb.t[:, j, 0:nq], in0=obs[0][j].t[:, 0:nq], in1=rd0.t[:, 0:nq],
                                                                          op=ALU.mult), reads=[obs[0][j].res, rd0.res], writes=[ocb.res])
                            tr.op(tr.pool, lambda: nc.gpsimd.tensor_tensor(out=ocb.t[:, j, 0:nq], in0=ocb.t[:, j, 0:nq], in1=ta.t[:, 0:nq],
                                                                           op=ALU.add), reads=[ocb.res, ta.res], writes=[ocb.res])
                            tr.op(tr.act, lambda: nc.scalar.activation(out=sqb.t[:, j, 0:nq], in_=ocb.t[:, j, 0:nq], func=AF.Square),
                                  reads=[ocb.res], writes=[sqb.res])
                        sb_ = sc.next()
                        tr.op(tr.pe, lambda: [nc.tensor.matmul(sb_.t[:, 0:nq], self.onesb.t[:], sqb.t[:, j, 0:nq],
                                                               start=(j == 0), stop=(j == 1)) for j in range(2)],
                              reads=[sqb.res, self.onesb.res], writes=[sb_.res])
                        self.rstd_from_psum(rsb.t[:, 0:nq], sb_.t[:, 0:nq], 256, [sb_.res], [rsb.res])
                        for j in range(2):
                            og = st["ostg"].next()
                            tr.op(tr.dve, lambda: nc.vector.scalar_tensor_tensor(
                                out=og.t[:, 0:nq], in0=ocb.t[:, j, 0:nq], scalar=sub.t[:, j:j + 1], in1=rsb.t[:, 0:nq],
                                op0=ALU.mult, op1=ALU.mult), reads=[ocb.res, sub.res, rsb.res], writes=[og.res])
                            tr.dma(tr.pool, og.res, [(self.attnT[r + j * 128:r + (j + 1) * 128, tb + q0:tb + q0 + nq],
                                                      og.t[:, 0:nq])], reads=[og.res])
            tr.barrier()

    def phase_attn_nbr(self, slot, with_ctx):
        cfg, tr, nc, I = self.cfg, self.tr, self.nc, self.I
        B, S, CTX, NB, D, ROWS = cfg.B, cfg.S, cfg.CTX, cfg.NB, cfg.D, cfg.ROWS
        nbk = NB // 128
        scale = 128 ** -0.5
        wr_ = min(8, ROWS)
        NR = 8
        with ExitStack() as es:
            st = self.attn_common_alloc(es, nbk)
            qTs = Ring([self.sb(es, [128, NB], BF16, "qT", slot=True) for _ in range(2)])
            TB = Ring([self.sb(es, [128, 17, 64], F32, "TB", slot=True) for _ in range(2)])
            cm = self.sb(es, [128, 64], F32, "cm", slot=True)
            rm = self.sb(es, [128, 4], F32, "rm", slot=True)
            sc = Ring([self.ps(es, "sc") for _ in range(3)])
            tmpw = Ring([self.sb(es, [128, 512], F32, "tmpw") for _ in range(3)])
            pr = Ring([self.sb(es, [128, 512], BF16, "pT") for _ in range(4)])
            ob = Ring([self.ps(es, "ob") for _ in range(2)])
            db = Ring([self.ps(es, "db") for _ in range(2)])
            tr.dma(tr.sp, cm.res, [(cm.t[:], I["k_cm"][:, :])], writes=[cm.res])
            tr.dma(tr.sp, rm.res, [(rm.t[:], I["k_rm"][:, :])], writes=[rm.res])
            for h in range(cfg.BH):
                tbt = TB.next()
                base = (slot * cfg.BH + h) * 18
                tr.dma(tr.sp, tbt.res,
                       [(tbt.t[0:64, :, :], I["rpbx"][base:base + 17, :].rearrange("e (k q) -> k e q", q=64)),
                        (tbt.t[64:128, :, :], I["rpbx"][base + 1:base + 18, :].rearrange("e (k q) -> k e q", q=64))],
                       writes=[tbt.res])
                for e in range(17):
                    tr.op(tr.pool, lambda: nc.gpsimd.tensor_tensor(out=tbt.t[:, e, :], in0=tbt.t[:, e, :], in1=cm.t[:], op=ALU.add),
                          reads=[tbt.res, cm.res], writes=[tbt.res])
                for b in range(B):
                    tb = b * NB
                    kT, v, qT = st["kT"].next(), st["v"].next(), qTs.next()
                    tr.dma(tr.sp, kT.res, [(kT.t[:], self.qkvT[D + h * 128:D + (h + 1) * 128, tb:tb + NB])], writes=[kT.res])
                    tr.dma(tr.sp, qT.res, [(qT.t[:], self.qkvT[h * 128:(h + 1) * 128, tb:tb + NB])], writes=[qT.res])
                    self.load_vT_transposed(st, v, 2 * D + h * 128, tb, nbk)
                    for r8 in range(0, ROWS, NR):
                        r0s = [min(max(r - 4, 0), ROWS - wr_) for r in range(r8, r8 + NR)]
                        jmin, jmax = min(r0s) // 2, (max(r0s) + wr_ - 1) // 2
                        kts = [0, 1] + [2 + j for j in range(jmin, jmax + 1)]

                        def exp_fn(idx, kt, sb_, pb):
                            if kt < 2:
                                return False
                            j = kt - 2
                            tw_ = tmpw.next()
                            ri = 0
                            while ri < NR:
                                r, r0 = r8 + ri, r0s[ri]
                                top_ok = r0 <= 2 * j < r0 + wr_
                                bot_ok = r0 <= 2 * j + 1 < r0 + wr_
                                if not (top_ok or bot_ok):
                                    rj = ri
                                    while rj + 1 < NR:
                                        r0n = r0s[rj + 1]
                                        if (r0n <= 2 * j < r0n + wr_) or (r0n <= 2 * j + 1 < r0n + wr_):
                                            break
                                        rj += 1
                                    cs = slice(ri * 64, (rj + 1) * 64)
                                    tr.op(tr.act, lambda: nc.scalar.activation(out=pb.t[:, cs], in_=sb_.t[:, cs], func=AF.Exp,
                                                                               bias=rm.t[:, 3:4], scale=scale),
                                          reads=[sb_.res, rm.res, pb.res], writes=[pb.res])
                                    ri = rj + 1
                                    continue
                                def mc_of(rr0):
                                    t_ok = rr0 <= 2 * j < rr0 + wr_
                                    b_ok = rr0 <= 2 * j + 1 < rr0 + wr_
                                    return None if not (t_ok or b_ok) else (0 if (t_ok and b_ok) else (1 if t_ok else 2))
                                mcol = mc_of(r0)
                                rj = ri
                                while rj + 1 < NR and mc_of(r0s[rj + 1]) == mcol:
                                    rj += 1
                                for rk in range(ri, rj + 1):
                                    e = 2 * j - (r8 + rk) + 8
                                    ck = slice(rk * 64, (rk + 1) * 64)
                                    tr.op(tr.dve, lambda: nc.vector.scalar_tensor_tensor(
                                        out=tw_.t[:, ck], in0=sb_.t[:, ck], scalar=scale, in1=tbt.t[:, e, :],
                                        op0=ALU.mult, op1=ALU.add), reads=[sb_.res, tbt.res, tw_.res], writes=[tw_.res])
                                cs = slice(ri * 64, (rj + 1) * 64)
                                tr.op(tr.act, lambda: nc.scalar.activation(out=pb.t[:, cs], in_=tw_.t[:, cs], func=AF.Exp,
                                                                           bias=rm.t[:, mcol:mcol + 1], scale=1.0),
                                      reads=[tw_.res, rm.res, pb.res], writes=[pb.res])
                                ri = rj + 1
                            return True

                        o_b, d_b = ob.next(), db.next()
                        q0 = CTX + r8 * 64
                        self.attend_block(st, qT.t[:, q0:q0 + NR * 64], qT.res, NR * 64, kT, v, kts, sc, pr, o_b, d_b, scale,
                                          exp_fn=exp_fn)
                        self.finish_o(st, o_b, d_b, NR * 64, h * 128, tb + q0)
                    if with_ctx:
                        o_b, d_b = ob.next(), db.next()
                        self.attend_block(st, qT.t[:, 0:CTX], qT.res, CTX, kT, v, list(range(CTX // 128)), sc, pr, o_b, d_b, scale)
                        self.finish_o(st, o_b, d_b, CTX, h * 128, tb)
            tr.barrier()

    def layer_steps(self, i):
        cfg = self.cfg
        D, HID = cfg.D, cfg.HID
        kind, slot = i % 3, i // 3
        with_ctx = i < cfg.DEPTH - 1
        steps = [lambda: self.phase_mod(i), lambda: self.phase_norm(self.gs1, 0, self.uT)]

        def qkv():
            alloc, epi = self.make_epi_qkv(kind, slot)
            self.gemm(self.uT, self.W[i]["qkv"], D, cfg.NQKV[kind], epi, alloc)
        steps.append(qkv)
        if kind == 0:
            steps.append(lambda: self.phase_attn_gqa(with_ctx))
        elif kind == 1:
            steps.append(lambda: self.phase_attn_nbr(slot, with_ctx))
        else:
            steps.append(lambda: self.phase_attn_diff(slot, 0.8 - 0.6 * math.exp(-0.3 * i), with_ctx))

        def oproj():
            alloc, epi = self.make_epi_resid(2)
            self.gemm(self.attnT, self.W[i]["o"], D, D, epi, alloc)
        steps.append(oproj)
        steps.append(lambda: self.phase_norm(self.gs2, 3, self.uT))

        def up():
            alloc, epi = self.make_epi_relu2(self.HT)
            self.gemm(self.uT, self.W[i]["w1"], D, HID, epi, alloc)
        steps.append(up)
        for j in range(HID // D):
            def down(j=j):
                alloc, epi = self.make_epi_resid(5)
                self.gemm(self.HT.aps[j], self.W[i]["w2"][j], D, D, epi, alloc)
            steps.append(down)
        return steps

    def phase_final(self):
        cfg, tr, nc = self.cfg, self.tr, self.nc
        D, KC, NB, B, S, CTX = cfg.D, cfg.KC, cfg.NB, cfg.B, cfg.S, cfg.CTX
        TW = 256
        with ExitStack() as es:
            hin = [self.sb(es, [128, KC, TW], F32, "hin", slot=True) for _ in range(2)]
            sq = [self.sb(es, [128, KC, TW], BF16, "sq") for _ in range(2)]
            yv = [self.sb(es, [128, TW], F32, "yv") for _ in range(4)]
            rs = [self.sb(es, [128, TW], F32, "rstd") for _ in range(2)]
            ot = [self.sb(es, [128, D], F32, "ot", slot=True) for _ in range(2)]
            pss = [self.ps(es, "nps") for _ in range(2)]
            tps = [self.ps(es, "ftp") for _ in range(4)]
            hr, sqr, yr, rsr, otr, pr, tpr = Ring(hin), Ring(sq), Ring(yv), Ring(rs), Ring(ot), Ring(pss), Ring(tps)
            k = 0
            for b in range(B):
                for s0 in range(0, S, TW):
                    t0 = b * NB + CTX + s0
                    hb, sb_, rb, pb = hr.next(), sqr.next(), rsr.next(), pr.next()
                    tr.dma(tr.sp, hb.res, self.chunk_pairs(hb.t, self.hT, t0, TW, True), writes=[hb.res])
                    tr.op(tr.act, lambda: nc.scalar.activation(out=sb_.t[:], in_=hb.t[:], func=AF.Square),
                          reads=[hb.res], writes=[sb_.res])
                    tr.op(tr.pe, lambda: [nc.tensor.matmul(pb.t[:, 0:TW], self.onesb.t[:], sb_.t[:, kk, :],
                                                           start=(kk == 0), stop=(kk == KC - 1)) for kk in range(KC)],
                          reads=[sb_.res, self.onesb.res], writes=[pb.res])
                    self.rstd_from_psum(rb.t[:], pb.t[:, 0:TW], D, [pb.res], [rb.res])
                    obs = [otr.next() for _ in range(TW // 128)]
                    for c0 in range(0, KC, 4):
                        tp = [tpr.next() for _ in range(TW // 128)]
                        for c in range(c0, min(c0 + 4, KC)):
                            yb = yr.next()
                            tr.op(tr.dve, lambda: nc.vector.scalar_tensor_tensor(
                                out=yb.t[:], in0=hb.t[:, c, :], scalar=self.fgT.t[:, c:c + 1], in1=rb.t[:],
                                op0=ALU.mult, op1=ALU.mult), reads=[hb.res, rb.res, self.fgT.res], writes=[yb.res])
                            for j in range(TW // 128):
                                tr.op(tr.pe, lambda: nc.tensor.transpose(out=tp[j].t[:, (c - c0) * 128:(c - c0 + 1) * 128],
                                                                         in_=yb.t[:, j * 128:(j + 1) * 128],
                                                                         identity=self.ident.t[:]),
                                      reads=[yb.res, self.ident.res, tp[j].res], writes=[tp[j].res])
                        ncols = (min(c0 + 4, KC) - c0) * 128
                        for j in range(TW // 128):
                            k += 1
                            if k % 2:
                                tr.op(tr.act, lambda: nc.scalar.copy(out=obs[j].t[:, c0 * 128:c0 * 128 + ncols], in_=tp[j].t[:, 0:ncols]),
                                      reads=[tp[j].res, obs[j].res], writes=[obs[j].res])
                            else:
                                tr.op(tr.dve, lambda: nc.vector.tensor_copy(out=obs[j].t[:, c0 * 128:c0 * 128 + ncols], in_=tp[j].t[:, 0:ncols]),
                                      reads=[tp[j].res, obs[j].res], writes=[obs[j].res])
                    for j in range(TW // 128):
                        r0 = b * S + s0 + j * 128
                        tr.dma(tr.pool, obs[j].res, [(self.out[r0:r0 + 128, :], obs[j].t[:])], reads=[obs[j].res])
            tr.barrier()


def host_constants(cfg):
    NTOK, NB, CTX, S, GW = cfg.NTOK, cfg.NB, cfg.CTX, cfg.S, cfg.GW
    k = {}
    k["k_ident"] = np.eye(128, dtype=np.float32)
    k["k_ones"] = np.ones((128, 128), np.float32)
    sw = np.zeros((128, 128), np.float32)
    for p in range(128):
        sw[p, p ^ 1] = 1.0
    k["k_swap"] = sw
    t = np.arange(S)
    row = (t // GW).astype(np.float32)
    col = (t % GW).astype(np.float32)
    nf = 128 // 4
    inv = (np.float32(10000.0) ** (-np.arange(nf, dtype=np.float32) / np.float32(nf))).astype(np.float32)
    ang = np.concatenate([row[:, None] * inv, col[:, None] * inv], axis=-1).astype(np.float32)
    cos, sin = np.cos(ang).astype(np.float32), np.sin(ang).astype(np.float32)
    cosT = np.ones((128, NB), np.float32)
    sinT = np.zeros((128, NB), np.float32)
    cosT[0::2, CTX:] = cos.T
    cosT[1::2, CTX:] = cos.T
    sinT[0::2, CTX:] = -sin.T
    sinT[1::2, CTX:] = sin.T
    k["k_cos"] = np.ascontiguousarray(np.tile(cosT, (1, cfg.B)))
    k["k_sin"] = np.ascontiguousarray(np.tile(sinT, (1, cfg.B)))
    qc = np.arange(64)
    c0 = np.clip(qc - 8, 0, 64 - 16)
    kc = np.arange(64)
    inw = (kc[:, None] >= c0[None, :]) & (kc[:, None] < c0[None, :] + 16)
    cm = np.where(inw, 0.0, NEG).astype(np.float32)
    k["k_cm"] = np.concatenate([cm, cm], axis=0)
    rm = np.zeros((128, 4), np.float32)
    rm[64:, 1] = NEG
    rm[:64, 2] = NEG
    rm[:, 3] = NEG
    k["k_rm"] = rm
    return k


def host_inputs(cfg, inp):
    D, B, S, CTX, DEPTH = cfg.D, cfg.B, cfg.S, cfg.CTX, cfg.DEPTH
    f = lambda a: np.ascontiguousarray(np.asarray(a, dtype=np.float32))
    m = {}
    m["x"] = f(inp["x"]).reshape(B * S, D)
    m["c"] = f(inp["c"])
    m["ctx"] = f(inp["ctx"]).reshape(B * CTX, D)
    m["c_ctx"] = f(inp["c_ctx"]).reshape(1, D)
    m["norm1_g"] = f(inp["norm1_g"]); m["norm2_g"] = f(inp["norm2_g"])
    m["mod_down"] = f(inp["mod_down"]).reshape(-1, cfg.RANK)
    m["mod_up"] = f(inp["mod_up"]).reshape(-1, 6 * D)
    m["mod_b"] = f(inp["mod_b"])
    m["mlp_w1"] = f(inp["mlp_w1"]).reshape(-1, cfg.HID)
    m["mlp_w2"] = f(inp["mlp_w2"]).reshape(-1, D)
    m["a_w_qkv"] = f(inp["a_w_qkv"]).reshape(-1, cfg.NQKV[0])
    m["a_w_o"] = f(inp["a_w_o"]).reshape(-1, D)
    m["a_q_g"] = f(inp["a_q_g"]); m["a_k_g"] = f(inp["a_k_g"])
    m["b_w_qkv"] = f(inp["b_w_qkv"]).reshape(-1, 3 * D)
    m["b_w_o"] = f(inp["b_w_o"]).reshape(-1, D)
    rpb = f(inp["b_rpb"])
    L, H = rpb.shape[0], rpb.shape[1]
    kc = np.arange(64)[:, None]
    qc = np.arange(64)[None, :]
    dc = np.clip(kc - qc + 15, 0, 30)
    tz = rpb[:, :, :, dc]
    rpbx = np.zeros((L, H, 18, 64, 64), np.float32)
    rpbx[:, :, 1:16] = tz
    m["rpbx"] = rpbx.reshape(L * H * 18, 64 * 64)
    m["c_w_qkv"] = f(inp["c_w_qkv"]).reshape(-1, 3 * D)
    m["c_w_o"] = f(inp["c_w_o"]).reshape(-1, D)
    m["c_lam"] = np.ascontiguousarray(np.stack([f(inp["c_lam_q1"]), f(inp["c_lam_k1"]), f(inp["c_lam_q2"]), f(inp["c_lam_k2"])],
                                               axis=1).reshape(-1, 128))
    m["c_subln_g"] = f(inp["c_subln_g"])
    m["final_g"] = f(inp["final_g"]).reshape(1, D)
    m.update(host_constants(cfg))
    return m


_CACHE = {}
NCORES = 2


def run(cfg_total, inputs, ncores=NCORES):
    import os
    B = cfg_total.B
    assert B % ncores == 0
    bc = B // ncores
    cfg = Cfg(D=cfg_total.D, B=bc, S=cfg_total.S, CTX=cfg_total.CTX, DEPTH=cfg_total.DEPTH, HID=cfg_total.HID,
              RANK=cfg_total.RANK, GW=cfg_total.GW)
    cfg.debug = getattr(cfg_total, "debug", False)
    key = (cfg.D, cfg.B, cfg.S, cfg.HID, cfg.RANK, os.environ.get("KSTOP"), cfg.debug)
    if key not in _CACHE:
        _CACHE[key] = Prog(cfg).build()
    nc = _CACHE[key]
    shared = None
    maps = []
    for c in range(ncores):
        sub = dict(inputs)
        for k in ("x", "c", "ctx"):
            sub[k] = np.asarray(inputs[k])[c * bc:(c + 1) * bc]
        m = host_inputs(cfg, sub)
        if shared is None:
            shared = m
        else:
            for k in m:
                if k not in ("x", "c", "ctx"):
                    m[k] = shared[k]
        maps.append(m)
    res = run_bass_kernel_spmd(nc, maps, core_ids=list(range(ncores)))
    if cfg.debug:
        cfg_total.dbg = res.results[0]
    outs = [np.asarray(res.results[c]["out"]).reshape(bc, cfg.S, cfg.D) for c in range(ncores)]
    return np.concatenate(outs, axis=0)


def kernel(**inputs):
    return run(Cfg(), inputs)
```
